# Optimizing a Trainium2 kernel written in Bass

```python
import math
import jax, jax.numpy as jnp
from jax import lax
import numpy as np

D_MODEL = 1024
BATCH = 8
SEQ = 2048
DEPTH = 1

MEM_LEN = 256
EPS = 1e-6
RET_HEADS = 4
RET_DK = 128
RET_DV = 256
CHUNK = 128
ROPE_BASE = 10000.0
Q_W = RET_HEADS * RET_DK
V_W = RET_HEADS * RET_DV
LRU_WIDTH = 1024
LRU_BLOCKS = 8
LRU_BLOCK = LRU_WIDTH // LRU_BLOCKS
CONV_W = 4
LRU_C = 8.0
IN_W = 2 * Q_W + 2 * V_W + 2 * LRU_WIDTH
N_BRANCH = 2
D_FF = 2816
X_HEADS = 4
X_HD = D_MODEL // X_HEADS

kernel_name = "hybrid_retention_rglru_macaron_xattn"


def rmsnorm(x, g):
    xf = x.astype(jnp.float32)
    y = xf * lax.rsqrt(jnp.mean(xf * xf, axis=-1, keepdims=True) + EPS)
    return (y * g.astype(jnp.float32)).astype(x.dtype)


def swiglu(h, w1, w3, w2):
    return (jax.nn.silu(h @ w1) * (h @ w3)) @ w2


def rotary(t, pos):
    dk = t.shape[-1]
    inv_freq = ROPE_BASE ** (-jnp.arange(0, dk, 2, dtype=jnp.float32) / dk)
    ang = pos[:, None] * inv_freq[None, :]
    cos = jnp.cos(ang)[None, :, None, :]
    sin = jnp.sin(ang)[None, :, None, :]
    t1, t2 = t[..., : dk // 2], t[..., dk // 2:]
    return jnp.concatenate([t1 * cos - t2 * sin, t2 * cos + t1 * sin], axis=-1)


def retention_chunkwise(q, k, v):
    B, S, H, dk = q.shape
    dv = v.shape[-1]
    nC = S // CHUNK
    log_gamma = jnp.log(1.0 - 2.0 ** (-5.0 - jnp.arange(H, dtype=jnp.float32)))
    q = q.reshape(B, nC, CHUNK, H, dk)
    k = k.reshape(B, nC, CHUNK, H, dk) * (dk ** -0.5)
    v = v.reshape(B, nC, CHUNK, H, dv)
    pos = jnp.arange(CHUNK, dtype=jnp.float32)
    rel = pos[:, None] - pos[None, :]
    decay = jnp.where(rel[None] >= 0, jnp.exp(rel[None] * log_gamma[:, None, None]), 0.0)
    scores = jnp.einsum('bnihd,bnjhd->bnhij', q, k) * decay[None, None]
    inner = jnp.einsum('bnhij,bnjhe->bnihe', scores, v)
    k_decay = jnp.exp((CHUNK - 1.0 - pos)[None, :] * log_gamma[:, None])
    kv = jnp.einsum('bnjhd,bnjhe,hj->nbhde', k, v, k_decay)
    chunk_decay = jnp.exp(CHUNK * log_gamma)[None, :, None, None]

    def step(state, kv_c):
        return chunk_decay * state + kv_c, state

    init = jnp.zeros((B, H, dk, dv), jnp.float32)
    _, prev = lax.scan(step, init, kv)
    prev = jnp.moveaxis(prev, 0, 1)
    q_decay = jnp.exp((pos + 1.0)[:, None] * log_gamma[None, :])
    cross = jnp.einsum('bnihd,bnhde->bnihe', q, prev) * q_decay[None, None, :, :, None]
    return (inner + cross).reshape(B, S, H, dv)


def head_groupnorm(y, g):
    mu = jnp.mean(y, axis=-1, keepdims=True)
    var = jnp.mean(jnp.square(y - mu), axis=-1, keepdims=True)
    yn = (y - mu) * lax.rsqrt(var + EPS)
    B, S, H, dv = y.shape
    return yn.reshape(B, S, H * dv) * g.astype(jnp.float32)


def causal_depthwise_conv(x, w, b):
    S = x.shape[1]
    xp = jnp.pad(x, ((0, 0), (CONV_W - 1, 0), (0, 0)))
    y = xp[:, 0:S] * w[0]
    for tap in range(1, CONV_W):
        y = y + xp[:, tap:tap + S] * w[tap]
    return y + b


def rg_lru(x, w_r, b_r, w_i, b_i, lam):
    B, S, W = x.shape
    xb = x.reshape(B, S, LRU_BLOCKS, LRU_BLOCK)
    r = jax.nn.sigmoid(jnp.einsum('bsgi,gij->bsgj', xb, w_r).reshape(B, S, W) + b_r)
    i = jax.nn.sigmoid(jnp.einsum('bsgi,gij->bsgj', xb, w_i).reshape(B, S, W) + b_i)
    log_a = -LRU_C * r * jax.nn.softplus(-lam)
    a = jnp.exp(log_a)
    mult = jnp.sqrt(-jnp.expm1(2.0 * log_a))
    bx = mult * (i * x)

    def combine(c1, c2):
        a1, b1 = c1
        a2, b2 = c2
        return a1 * a2, a2 * b1 + b2

    _, h = lax.associative_scan(combine, (a, bx), axis=1)
    return h


def setup_inputs(seed: int = 0) -> dict:
    key = jax.random.key(seed)
    ks = iter(jax.random.split(key, 64))

    def nrm(shape, scale):
        return jax.random.normal(next(ks), shape, jnp.float32) * scale

    def gain(shape):
        return 1.0 + nrm(shape, 0.02)

    L, D = DEPTH, D_MODEL
    lam_a = jax.random.uniform(next(ks), (L, LRU_WIDTH), jnp.float32, 0.9, 0.999)
    return {
        "x": nrm((BATCH, SEQ, D), 1.0),
        "mem": nrm((BATCH, MEM_LEN, D), 1.0),
        "ffn1_norm": gain((L, D)),
        "ffn1_w1": nrm((L, D, D_FF), D ** -0.5),
        "ffn1_w3": nrm((L, D, D_FF), D ** -0.5),
        "ffn1_w2": nrm((L, D_FF, D), D_FF ** -0.5),
        "mix_norm": gain((L, D)),
        "w_in": nrm((L, D, IN_W), D ** -0.5),
        "ret_gn": gain((L, V_W)),
        "w_ret_o": nrm((L, V_W, D), V_W ** -0.5),
        "conv_w": nrm((L, CONV_W, LRU_WIDTH), CONV_W ** -0.5),
        "conv_b": nrm((L, LRU_WIDTH), 0.01),
        "w_rgate": nrm((L, LRU_BLOCKS, LRU_BLOCK, LRU_BLOCK), LRU_BLOCK ** -0.5),
        "b_rgate": nrm((L, LRU_WIDTH), 0.01),
        "w_igate": nrm((L, LRU_BLOCKS, LRU_BLOCK, LRU_BLOCK), LRU_BLOCK ** -0.5),
        "b_igate": nrm((L, LRU_WIDTH), 0.01),
        "lru_lambda": jnp.log(lam_a) - jnp.log1p(-lam_a),
        "w_lru_o": nrm((L, LRU_WIDTH, D), LRU_WIDTH ** -0.5),
        "w_branch_gate": nrm((L, D, N_BRANCH * D), D ** -0.5),
        "b_branch_gate": nrm((L, N_BRANCH * D), 0.01),
        "w_out": nrm((L, D, D), D ** -0.5),
        "xattn_norm": gain((L, D)),
        "mem_norm": gain((L, D)),
        "w_xq": nrm((L, D, D), D ** -0.5),
        "w_xk": nrm((L, D, D), D ** -0.5),
        "w_xv": nrm((L, D, D), D ** -0.5),
        "w_xo": nrm((L, D, D), D ** -0.5),
        "ffn2_norm": gain((L, D)),
        "ffn2_w1": nrm((L, D, D_FF), D ** -0.5),
        "ffn2_w3": nrm((L, D, D_FF), D ** -0.5),
        "ffn2_w2": nrm((L, D_FF, D), D_FF ** -0.5),
        "final_norm": gain((D,)),
    }


def reference(x, mem, ffn1_norm, ffn1_w1, ffn1_w3, ffn1_w2, mix_norm, w_in, ret_gn, w_ret_o,
              conv_w, conv_b, w_rgate, b_rgate, w_igate, b_igate, lru_lambda, w_lru_o,
              w_branch_gate, b_branch_gate, w_out, xattn_norm, mem_norm, w_xq, w_xk, w_xv, w_xo,
              ffn2_norm, ffn2_w1, ffn2_w3, ffn2_w2, final_norm):
    B, S, D = x.shape
    M = mem.shape[1]
    pos = jnp.arange(S, dtype=jnp.float32)
    for l in range(DEPTH):
        x = x + 0.5 * swiglu(rmsnorm(x, ffn1_norm[l]), ffn1_w1[l], ffn1_w3[l], ffn1_w2[l])

        h = rmsnorm(x, mix_norm[l])
        u = h @ w_in[l]
        o1 = Q_W
        o2 = o1 + Q_W
        o3 = o2 + V_W
        o4 = o3 + V_W
        o5 = o4 + LRU_WIDTH
        q = u[..., :o1].reshape(B, S, RET_HEADS, RET_DK).astype(jnp.float32)
        k = u[..., o1:o2].reshape(B, S, RET_HEADS, RET_DK).astype(jnp.float32)
        v = u[..., o2:o3].reshape(B, S, RET_HEADS, RET_DV).astype(jnp.float32)
        g_ret = u[..., o3:o4]
        x_lru = u[..., o4:o5]
        g_lru = u[..., o5:]

        ret = retention_chunkwise(rotary(q, pos), rotary(k, pos), v)
        ret = head_groupnorm(ret, ret_gn[l])
        y_ret = (jax.nn.silu(g_ret.astype(jnp.float32)) * ret).astype(x.dtype) @ w_ret_o[l]

        xc = causal_depthwise_conv(x_lru, conv_w[l], conv_b[l]).astype(jnp.float32)
        hl = rg_lru(xc, w_rgate[l].astype(jnp.float32), b_rgate[l].astype(jnp.float32),
                    w_igate[l].astype(jnp.float32), b_igate[l].astype(jnp.float32),
                    lru_lambda[l].astype(jnp.float32))
        y_lru = (hl * jax.nn.gelu(g_lru.astype(jnp.float32))).astype(x.dtype) @ w_lru_o[l]

        gates = jax.nn.sigmoid(h @ w_branch_gate[l] + b_branch_gate[l])
        merged = gates[..., :D] * y_ret + gates[..., D:] * y_lru
        x = x + merged @ w_out[l]

        hq = rmsnorm(x, xattn_norm[l])
        m = rmsnorm(mem, mem_norm[l])
        xq = (hq @ w_xq[l]).reshape(B, S, X_HEADS, X_HD)
        xk = (m @ w_xk[l]).reshape(B, M, X_HEADS, X_HD)
        xv = (m @ w_xv[l]).reshape(B, M, X_HEADS, X_HD)
        sc = jnp.einsum('bshd,bmhd->bhsm', xq.astype(jnp.float32), xk.astype(jnp.float32)) * (X_HD ** -0.5)
        p = jax.nn.softmax(sc, axis=-1)
        xo = jnp.einsum('bhsm,bmhd->bshd', p, xv.astype(jnp.float32)).reshape(B, S, D).astype(x.dtype)
        x = x + xo @ w_xo[l]

        x = x + 0.5 * swiglu(rmsnorm(x, ffn2_norm[l]), ffn2_w1[l], ffn2_w3[l], ffn2_w2[l])
    return rmsnorm(x, final_norm)
```

```python
import math
import contextlib
import numpy as np
import concourse.bass as bass
import concourse.mybir as mybir
from concourse.bass_utils import run_bass_kernel_spmd

F32 = mybir.dt.float32
BF16 = mybir.dt.bfloat16
I32 = mybir.dt.int32
AF = mybir.ActivationFunctionType
ALU = mybir.AluOpType

ENGS = ("pe", "act", "dve", "pool", "sp")
NDMASEM = 8


class H:
    __slots__ = ("name", "lw", "lr", "dead")

    def __init__(self, name=""):
        self.name = name
        self.lw = None
        self.lr = {}
        self.dead = False


class Op:
    __slots__ = ("eng", "fn", "dma", "id", "deps", "marked", "idx", "sem", "semval", "prev")

    def __init__(self, eng, fn, dma, id):
        self.eng = eng
        self.fn = fn
        self.dma = dma
        self.id = id
        self.deps = []
        self.marked = False
        self.idx = 0
        self.sem = None
        self.semval = 0
        self.prev = None


class Prog:
    def __init__(self):
        self.ops = []

    def add(self, eng, fn, reads=(), writes=(), dma=False):
        op = Op(eng, fn, dma, len(self.ops))
        deps = {}

        def need(d):
            if d is None:
                return
            if (not d.dma) and (not dma) and d.eng == "pe" and eng == "pe":
                return
            key = ("dma", d.id) if d.dma else d.eng
            if key not in deps or deps[key].id < d.id:
                deps[key] = d

        for h in reads:
            assert not h.dead, h.name
            need(h.lw)
        for h in writes:
            need(h.lw)
            for r in h.lr.values():
                need(r)
        rkey = ("dma", op.id) if dma else eng
        for h in reads:
            h.lr[rkey] = op
        for h in writes:
            h.lw = op
            h.lr = {}
        op.deps = list(deps.values())
        self.ops.append(op)
        return op

    def emit(self, nc):
        ops = self.ops
        for op in ops:
            for d in op.deps:
                d.marked = True
        cnt = {e: 0 for e in ENGS}
        dcnt = {e: [] for e in ENGS}
        for op in ops:
            if op.dma:
                lst = dcnt[op.eng]
                k = len(lst)
                op.idx = k
                op.prev = lst[k - NDMASEM] if k >= NDMASEM else None
                lst.append(op)
            elif op.marked:
                cnt[op.eng] += 1
                op.idx = cnt[op.eng]
        with contextlib.ExitStack() as st:
            esem = {e: st.enter_context(nc.semaphore("s_" + e)) for e in ENGS}
            dsem = {e: [st.enter_context(nc.semaphore("d_%s%d" % (e, i))) for i in range(NDMASEM)]
                    for e in ("sp", "pool")}
            for op in ops:
                if op.dma:
                    op.sem = dsem[op.eng][op.idx % NDMASEM]
                    op.semval = 16 * (op.idx // NDMASEM + 1)

            def run(ename, eng):
                waited = {}

                def wait(sem, val):
                    k = id(sem)
                    if waited.get(k, 0) >= val:
                        return
                    waited[k] = val
                    eng.wait_ge(sem, val)

                for op in ops:
                    if op.eng != ename:
                        continue
                    for d in op.deps:
                        if d.dma:
                            wait(d.sem, d.semval)
                        else:
                            wait(esem[d.eng], d.idx)
                    if op.dma and op.prev is not None:
                        wait(op.prev.sem, op.prev.semval)
                    ins = op.fn(eng) if op.fn is not None else None
                    if ins is None:
                        assert not op.marked and not op.dma
                        continue
                    if op.dma:
                        ins.then_inc(op.sem, 16)
                    elif op.marked:
                        ins.then_inc(esem[ename], 1)

            with nc.Block() as block:
                block.tensor(lambda e: run("pe", e))
                block.scalar(lambda e: run("act", e))
                block.vector(lambda e: run("dve", e))
                block.gpsimd(lambda e: run("pool", e))
                block.sync(lambda e: run("sp", e))


class View:
    __slots__ = ("ap", "hs")

    def __init__(self, ap, hs):
        self.ap = ap
        self.hs = hs


class Region:
    def __init__(self, nc, name, nbytes, gran=1024):
        self.t = nc.alloc_sbuf_tensor(name, [128, nbytes // 4], F32)
        self.h = [H("%s_%d" % (name, i)) for i in range((nbytes + gran - 1) // gran)]
        self.gran = gran
        self.nbytes = nbytes

    def view(self, off, dtype, *shape):
        es = 4 if dtype in (F32, I32) else 2
        n = es * int(np.prod(shape))
        assert off % 4 == 0 and off + n <= self.nbytes, (off, n, self.nbytes)
        ap = self.t[:, off // 4:(off + n) // 4]
        if dtype != F32:
            ap = ap.bitcast(dtype)
        if len(shape) == 2:
            ap = ap.rearrange("p (a b) -> p a b", a=shape[0])
        elif len(shape) == 3:
            ap = ap.rearrange("p (a b c) -> p a b c", a=shape[0], b=shape[1])
        return View(ap, self.h[off // self.gran:(off + n - 1) // self.gran + 1])


T = 2048
D = 1024
KC = 8
TB = 512
NTB = 4
DFF = 2816
EPS = 1e-6
MEM = 256
FFN_GROUPS = [(0, 6), (6, 6), (12, 5), (17, 5)]
RINGC = 18432
R3BYTES = 56 * 1024

WSHAPES = [
    ("ffn1_norm", [D]), ("ffn1_w1", [D, DFF]), ("ffn1_w3", [D, DFF]), ("ffn1_w2", [DFF, D]),
    ("mix_norm", [D]), ("w_in", [D, 5120]), ("ret_gn", [D]), ("w_ret_o", [D, D]),
    ("conv_w", [4, D]), ("conv_b", [D]), ("w_rgate", [8, 128, 128]), ("b_rgate", [D]),
    ("w_igate", [8, 128, 128]), ("b_igate", [D]), ("lru_lambda", [D]), ("w_lru_o", [D, D]),
    ("w_branch_gate", [D, 2 * D]), ("b_branch_gate", [2 * D]), ("w_out", [D, D]),
    ("xattn_norm", [D]), ("mem_norm", [D]), ("w_xq", [D, D]), ("w_xk", [D, D]),
    ("w_xv", [D, D]), ("w_xo", [D, D]), ("ffn2_norm", [D]), ("ffn2_w1", [D, DFF]),
    ("ffn2_w3", [D, DFF]), ("ffn2_w2", [DFF, D]), ("final_norm", [D]),
]

PCOL = {}
_r = 0
for _n, _k in [("ffn1_norm", 8), ("mix_norm", 8), ("xattn_norm", 8), ("ffn2_norm", 8),
               ("conv_w", 32), ("conv_b", 8), ("b_rgate", 8), ("b_igate", 8),
               ("lru_lambda", 8), ("b_branch_gate", 16)]:
    PCOL[_n] = _r
    _r += _k
NPCOL = _r


DEBUG = False
F_LEVEL = 4
SG_FUNC = AF.Silu
RET_LEVEL = 4
DBG_NAMES = []


def build(stage=99):
    nc = bass.Bass("TRN2", target_bir_lowering=False)
    P = Prog()

    def dram(name, shape, kind="ExternalInput"):
        return nc.dram_tensor(name, list(shape), F32, kind=kind).ap()

    x_d = dram("x", [T, D])
    mem_d = dram("mem", [MEM, D])
    Wd = {n: dram(n, s) for n, s in WSHAPES}
    out_d = dram("out", [T, D], kind="ExternalOutput")
    park_d = nc.dram_tensor("park", [128, KC * T], F32).ap()
    hpark = H("park")

    identf = nc.alloc_sbuf_tensor("identf", [128, 128], F32)
    identb = nc.alloc_sbuf_tensor("identb", [128, 128], BF16)
    onesb = nc.alloc_sbuf_tensor("onesb", [128, 128], BF16)
    io_i = nc.alloc_sbuf_tensor("io_i", [128, 128], I32)
    io_f = nc.alloc_sbuf_tensor("io_f", [128, 128], F32)
    prow = nc.alloc_sbuf_tensor("prow", [128, 128], F32)
    pc = nc.alloc_sbuf_tensor("pc", [128, 128], F32)
    bcrow = nc.alloc_sbuf_tensor("bcrow", [128, D], F32)
    small = nc.alloc_sbuf_tensor("small", [128, 256], F32)
    wgate = nc.alloc_sbuf_tensor("wgate", [128, 2048], BF16)
    decT = nc.alloc_sbuf_tensor("decT", [128, 512], F32)
    qdec = nc.alloc_sbuf_tensor("qdec", [128, 512], F32)
    tmpc = nc.alloc_sbuf_tensor("tmpc", [128, 128], F32)
    h_identf, h_identb, h_ones, h_io, h_iof, h_prow, h_pc, h_bcrow = (H(n) for n in
        ("identf", "identb", "ones", "io", "iof", "prow", "pc", "bcrow"))
    ring = nc.alloc_sbuf_tensor("ring", [128, RINGC], BF16)
    R1 = Region(nc, "R1", 64 * 1024)
    R2 = Region(nc, "R2", 32 * 1024)
    R3 = Region(nc, "R3", R3BYTES)
    banks = [nc.alloc_psum_tensor("bank%d" % i, [128, 512], F32) for i in range(8)]
    hbk = [H("bank%d" % i) for i in range(8)]
    bstate = {"i": 0}

    def nextbank():
        i = bstate["i"]
        bstate["i"] = (i + 1) % 8
        return i

    xT = {(c, tb): R1.view((c * T + tb * TB) * 4, F32, TB) for c in range(KC) for tb in range(NTB)}
    hT = {(c, tb): R2.view((c * T + tb * TB) * 2, BF16, TB) for c in range(KC) for tb in range(NTB)}
    xT_all = R1.view(0, F32, KC, T)
    hT_all = R2.view(0, BF16, KC, T)

    rstate = {"ptr": 0, "live": []}

    def walloc(ncols):
        assert ncols <= RINGC
        if rstate["ptr"] + ncols > RINGC:
            rstate["ptr"] = 0
        a = rstate["ptr"]
        b = a + ncols
        ev = []
        keep = []
        for (s, e, h) in rstate["live"]:
            if s < b and a < e:
                h.dead = True
                ev.append(h)
            else:
                keep.append((s, e, h))
        h = H("w@%d" % a)
        keep.append((a, b, h))
        rstate["live"] = keep
        rstate["ptr"] = b
        return ring[:, a:b], h, ev

    def wload(src, *shape):
        n = int(np.prod(shape))
        ap, h, ev = walloc(n)
        if len(shape) == 2:
            dst = ap.rearrange("p (k f) -> p k f", k=shape[0])
        elif len(shape) == 3:
            dst = ap.rearrange("p (k r f) -> p k r f", k=shape[0], r=shape[1])
        else:
            dst = ap
        P.add("pool", lambda e, dst=dst, src=src: e.dma_start(out=dst, in_=src),
              reads=[], writes=[h] + ev, dma=True)
        return dst, h

    def wload_pair(src0, src1, k, x):
        ap, h, ev = walloc(k * 2 * x)
        dst = ap.rearrange("p (k r f) -> p k r f", k=k, r=2)
        P.add("pool", lambda e: e.dma_start(out=dst[:, :, 0, :], in_=src0), reads=[], writes=[h] + ev, dma=True)
        P.add("pool", lambda e: e.dma_start(out=dst[:, :, 1, :], in_=src1), reads=[], writes=[h], dma=True)
        return dst, h

    dbg_list = []

    def dbg(name, view, dtype):
        if not DEBUG:
            return
        shp = [128] + [int(s) for s in view.ap.shape[1:]]
        t = nc.dram_tensor("dbg_" + name, shp, dtype, kind="ExternalOutput").ap()
        hh = H("dbg_" + name)
        P.add("sp", lambda e: e.dma_start(out=t, in_=view.ap), reads=view.hs, writes=[hh], dma=True)
        dbg_list.append("dbg_" + name)

    def setup_consts():
        P.add("pool", lambda e: e.iota(io_i[:, :], [[1, 128]], base=0, channel_multiplier=-1),
              writes=[h_io])
        P.add("dve", lambda e: e.tensor_scalar(identf[:, :], io_i[:, :], 0.0, None, ALU.is_equal),
              reads=[h_io], writes=[h_identf])
        P.add("dve", lambda e: e.tensor_copy(identb[:, :], identf[:, :]), reads=[h_identf], writes=[h_identb])
        P.add("dve", lambda e: e.memset(onesb[:, :], 1.0), writes=[h_ones])
        for n in ("ffn1_norm", "mix_norm", "xattn_norm", "ffn2_norm", "conv_b", "b_rgate", "b_igate",
                  "lru_lambda", "b_branch_gate"):
            r0 = PCOL[n]
            src = Wd[n].rearrange("(c p) -> c p", p=128)
            nr = src.shape[0]
            P.add("sp", lambda e, r0=r0, nr=nr, src=src: e.dma_start(out=prow[r0:r0 + nr, :], in_=src),
                  writes=[h_prow], dma=True)
        r0 = PCOL["conv_w"]
        src = Wd["conv_w"].rearrange("k (c p) -> (k c) p", p=128)
        P.add("sp", lambda e, r0=r0, src=src: e.dma_start(out=prow[r0:r0 + 32, :], in_=src),
              writes=[h_prow], dma=True)
        b = nextbank()
        P.add("pe", lambda e, b=b: e.transpose(banks[b][:, 0:NPCOL], prow[0:NPCOL, :], identf[0:NPCOL, 0:NPCOL]),
              reads=[h_prow, h_identf], writes=[hbk[b]])
        P.add("dve", lambda e, b=b: e.tensor_copy(pc[:, 0:NPCOL], banks[b][:, 0:NPCOL]),
              reads=[hbk[b]], writes=[h_pc])

    def pcol(name, i):
        r = PCOL[name] + i
        return pc[:, r:r + 1]

    def load_x():
        stg = [R3.view(i * 4096, F32, D) for i in range(4)]
        for tt in range(16):
            s = stg[tt % 4]
            P.add("sp", lambda e, s=s, tt=tt: e.dma_start(out=s.ap, in_=x_d[tt * 128:(tt + 1) * 128, :]),
                  writes=s.hs, dma=True)
            for half in range(2):
                b = nextbank()

                def tr(e, s=s, b=b, half=half):
                    ins = None
                    for j in range(4):
                        c = half * 4 + j
                        ins = e.transpose(banks[b][:, j * 128:(j + 1) * 128], s.ap[:, c * 128:(c + 1) * 128],
                                          identf[:, :])
                    return ins
                P.add("pe", tr, reads=s.hs + [h_identf], writes=[hbk[b]])
                dst = xT_all.ap[:, half * 4:half * 4 + 4, tt * 128:(tt + 1) * 128]
                src = banks[b][:, :].rearrange("p (a b) -> p a b", a=4)
                whs = []
                for j in range(4):
                    whs += xT[(half * 4 + j, tt // 4)].hs
                if half == 0:
                    P.add("act", lambda e, dst=dst, src=src: e.activation(dst, src, AF.Copy),
                          reads=[hbk[b]], writes=whs)
                else:
                    P.add("dve", lambda e, dst=dst, src=src: e.tensor_copy(dst, src),
                          reads=[hbk[b]], writes=whs)

    def rmsnorm_T(gname):
        sq = [R3.view(40 * 1024 + c * 1024, BF16, TB) for c in range(KC)]
        for tb in range(NTB):
            for c in range(KC):
                P.add("act", lambda e, c=c, tb=tb: e.activation(sq[c].ap, xT[(c, tb)].ap, AF.Square),
                      reads=xT[(c, tb)].hs, writes=sq[c].hs)
            b = nextbank()

            def mm(e, b=b):
                ins = None
                for c in range(KC):
                    ins = e.matmul(banks[b][:, :], onesb[:, :], sq[c].ap, start=(c == 0), stop=(c == KC - 1))
                return ins
            rhs = []
            for c in range(KC):
                rhs += sq[c].hs
            P.add("pe", mm, reads=rhs + [h_ones], writes=[hbk[b]])
            P.add("act", lambda e, b=b: e.activation(banks[b][:, :], banks[b][:, :], AF.Ln, bias=small[:, 0:1],
                                                     scale=1.0 / D),
                  reads=[hbk[b], h_small], writes=[hbk[b]])
            P.add("act", lambda e, b=b: e.activation(banks[b][:, :], banks[b][:, :], AF.Exp, scale=-0.5),
                  reads=[hbk[b]], writes=[hbk[b]])
            for c in range(KC):
                P.add("dve", lambda e, c=c, tb=tb, b=b: e.scalar_tensor_tensor(
                    hT[(c, tb)].ap, xT[(c, tb)].ap, pcol(gname, c), banks[b][:, :], ALU.mult, ALU.mult),
                    reads=xT[(c, tb)].hs + [hbk[b], h_pc], writes=hT[(c, tb)].hs)

    h_small = H("small")

    def setup_small():
        P.add("dve", lambda e: e.memset(small[:, 0:1], EPS), writes=[h_small])

    def ffn(n1, n3, n2, gname):
        w1_d = Wd[n1].rearrange("(k p) f -> p k f", p=128)
        w3_d = Wd[n3].rearrange("(k p) f -> p k f", p=128)
        w2_d = Wd[n2].rearrange("(c p) f -> p c f", p=128)
        hid = {(m, tb): R3.view((m * T + tb * TB) * 2, BF16, TB) for m in range(6) for tb in range(NTB)}
        ssb = [R3.view(24 * 1024 + i * 2048, F32, TB) for i in range(NTB)]
        wt = {}

        def load13(g):
            c0, G = FFN_GROUPS[g]
            wt[("w1", g)] = wload(w1_d[:, :, c0 * 128:(c0 + G) * 128], KC, G * 128)
            wt[("w3", g)] = wload(w3_d[:, :, c0 * 128:(c0 + G) * 128], KC, G * 128)

        def load2(g):
            c0, G = FFN_GROUPS[g]
            wt[("w2", g)] = wload(w2_d[:, c0:c0 + G, :], G, D)

        load13(0)
        rmsnorm_T(gname)
        allh = []
        for c in range(KC):
            for tb in range(NTB):
                allh += hT[(c, tb)].hs
        for g, (c0, G) in enumerate(FFN_GROUPS):
            load2(g)
            w1t, w1h = wt[("w1", g)]
            w3t, w3h = wt[("w3", g)]
            for m in range(G):
                bs = []
                for wtile, wh in ((w1t, w1h), (w3t, w3h)):
                    bb = [nextbank() for _ in range(NTB)]
                    bs.append(bb)

                    def mm(e, wtile=wtile, bb=bb, m=m):
                        ins = None
                        for k in range(KC):
                            for tb in range(NTB):
                                ins = e.matmul(banks[bb[tb]][:, :], wtile[:, k, m * 128:(m + 1) * 128],
                                               hT[(k, tb)].ap, start=(k == 0), stop=(k == KC - 1))
                        return ins
                    P.add("pe", mm, reads=allh + [wh], writes=[hbk[i] for i in bb])
                ba, bb_ = bs
                for tb in range(NTB):
                    P.add("act", lambda e, tb=tb, ba=ba: e.activation(ssb[tb].ap, banks[ba[tb]][:, :], AF.Silu),
                          reads=[hbk[ba[tb]]], writes=ssb[tb].hs)
                for tb in range(NTB):
                    P.add("dve", lambda e, tb=tb, bb_=bb_, m=m: e.tensor_tensor(
                        hid[(m, tb)].ap, banks[bb_[tb]][:, :], ssb[tb].ap, ALU.mult),
                        reads=[hbk[bb_[tb]]] + ssb[tb].hs, writes=hid[(m, tb)].hs)
            if g + 1 < len(FFN_GROUPS):
                load13(g + 1)
            w2t, w2h = wt[("w2", g)]
            hidh = []
            for m in range(G):
                for tb in range(NTB):
                    hidh += hid[(m, tb)].hs
            for oc in range(KC):
                bb = [nextbank() for _ in range(NTB)]

                def mm2(e, bb=bb, oc=oc, G=G, w2t=w2t):
                    ins = None
                    for m in range(G):
                        for tb in range(NTB):
                            ins = e.matmul(banks[bb[tb]][:, :], w2t[:, m, oc * 128:(oc + 1) * 128],
                                           hid[(m, tb)].ap, start=(m == 0), stop=(m == G - 1))
                    return ins
                P.add("pe", mm2, reads=hidh + [w2h], writes=[hbk[i] for i in bb])
                for tb in range(NTB):
                    P.add("dve", lambda e, tb=tb, bb=bb, oc=oc: e.scalar_tensor_tensor(
                        xT[(oc, tb)].ap, banks[bb[tb]][:, :], 0.5, xT[(oc, tb)].ap, ALU.mult, ALU.add),
                        reads=[hbk[bb[tb]]] + xT[(oc, tb)].hs, writes=xT[(oc, tb)].hs)

    def store_out(final_norm):
        ost = [R3.view(i * 4096, F32, D) for i in range(2)]
        st = [R3.view(8192 + i * 64, F32, 16) for i in range(4)]
        junk = R3.view(8192 + 1024, BF16, TB)
        if final_norm:
            P.add("sp", lambda e: e.dma_start(out=bcrow[:, :], in_=bass.AP(Wd["final_norm"].tensor, 0, [[0, 128], [1, D]])),
                  writes=[h_bcrow], dma=True)
        outh = H("out")
        for tt in range(16):
            o = ost[tt % 2]
            s = st[tt % 4]
            bb = [nextbank(), nextbank()]
            for half in range(2):
                def tr(e, half=half, tt=tt, b=bb[half]):
                    ins = None
                    for j in range(4):
                        c = half * 4 + j
                        ins = e.transpose(banks[b][:, j * 128:(j + 1) * 128],
                                          xT_all.ap[:, c, tt * 128:(tt + 1) * 128], identf[:, :])
                    return ins
                rh = []
                for j in range(4):
                    rh += xT[(half * 4 + j, tt // 4)].hs
                P.add("pe", tr, reads=rh + [h_identf], writes=[hbk[bb[half]]])
            if final_norm:
                for half in range(2):
                    P.add("act", lambda e, half=half, b=bb[half], s=s: e.activation(
                        junk.ap, banks[b][:, :], AF.Square, accum_out=s.ap[:, half:half + 1]),
                        reads=[hbk[bb[half]]], writes=s.hs + junk.hs)
                P.add("dve", lambda e, s=s: e.tensor_tensor(s.ap[:, 2:3], s.ap[:, 0:1], s.ap[:, 1:2], ALU.add),
                      reads=s.hs, writes=s.hs)
                P.add("act", lambda e, s=s: e.activation(s.ap[:, 3:4], s.ap[:, 2:3], AF.Ln, bias=small[:, 0:1],
                                                         scale=1.0 / D), reads=s.hs + [h_small], writes=s.hs)
                P.add("act", lambda e, s=s: e.activation(s.ap[:, 4:5], s.ap[:, 3:4], AF.Exp, scale=-0.5),
                      reads=s.hs, writes=s.hs)
                for half in range(2):
                    P.add("dve", lambda e, half=half, b=bb[half], s=s, o=o: e.scalar_tensor_tensor(
                        o.ap[:, half * 512:(half + 1) * 512], banks[b][:, :], s.ap[:, 4:5],
                        bcrow[:, half * 512:(half + 1) * 512], ALU.mult, ALU.mult),
                        reads=[hbk[bb[half]], h_bcrow] + s.hs, writes=o.hs)
            else:
                P.add("act", lambda e, b=bb[0], o=o: e.activation(o.ap[:, 0:512], banks[b][:, :], AF.Copy),
                      reads=[hbk[bb[0]]], writes=o.hs)
                P.add("dve", lambda e, b=bb[1], o=o: e.tensor_copy(o.ap[:, 512:1024], banks[b][:, :]),
                      reads=[hbk[bb[1]]], writes=o.hs)
            P.add("sp", lambda e, o=o, tt=tt: e.dma_start(out=out_d[tt * 128:(tt + 1) * 128, :], in_=o.ap),
                  reads=o.hs, writes=[outh], dma=True)
        ohs = [H("o%d" % i) for i in range(1)]
        P.add("sp", None, reads=[outh])
        for op in [o for o in P.ops if o.dma and o.eng == "sp"][-16:]:
            P.ops[-1].deps.append(op)

    def allhs(d):
        out = []
        for v in d.values():
            out += v.hs
        return out

    def xattn():
        XH = 4
        scale = 1.0 / 16.0
        wk_t, wk_h = wload(Wd["w_xk"].rearrange("(k p) f -> p k f", p=128), KC, D)
        wv_t, wv_h = wload(Wd["w_xv"].rearrange("(k p) f -> p k f", p=128), KC, D)
        rmsnorm_T("xattn_norm")
        hTh = allhs(hT)
        P.add("sp", lambda e: e.dma_start(out=bcrow[:, :], in_=bass.AP(Wd["mem_norm"].tensor, 0, [[0, 128], [1, D]])),
              writes=[h_bcrow], dma=True)
        mstg = [R3.view(i * 4096, F32, D) for i in range(2)]
        mn = [R3.view(8192 + i * 2048, BF16, D) for i in range(2)]
        mT = R3.view(12 * 1024, BF16, KC, MEM)
        xkT = R3.view(16 * 1024, BF16, KC, MEM)
        xv = R3.view(20 * 1024, BF16, 2, D)
        junk = R3.view(24 * 1024, BF16, D)
        hs_m = H("memstat")
        for mt in range(2):
            s = mstg[mt]
            P.add("sp", lambda e, s=s, mt=mt: e.dma_start(out=s.ap, in_=mem_d[mt * 128:(mt + 1) * 128, :]),
                  writes=s.hs, dma=True)
            c0 = 16 + mt * 4
            P.add("act", lambda e, s=s, c0=c0: e.activation(junk.ap, s.ap, AF.Square, accum_out=small[:, c0:c0 + 1]),
                  reads=s.hs, writes=junk.hs + [hs_m])
            P.add("act", lambda e, c0=c0: e.activation(small[:, c0 + 1:c0 + 2], small[:, c0:c0 + 1], AF.Ln,
                                                       bias=small[:, 0:1], scale=1.0 / D),
                  reads=[hs_m, h_small], writes=[hs_m])
            P.add("act", lambda e, c0=c0: e.activation(small[:, c0 + 2:c0 + 3], small[:, c0 + 1:c0 + 2], AF.Exp, scale=-0.5),
                  reads=[hs_m], writes=[hs_m])
            P.add("dve", lambda e, s=s, c0=c0, mt=mt: e.scalar_tensor_tensor(
                mn[mt].ap, s.ap, small[:, c0 + 2:c0 + 3], bcrow[:, :], ALU.mult, ALU.mult),
                reads=s.hs + [hs_m, h_bcrow], writes=mn[mt].hs)
            b = nextbank()
            bv = banks[b][:, :].bitcast(BF16)

            def tr(e, mt=mt, bv=bv):
                ins = None
                for c in range(KC):
                    ins = e.transpose(bv[:, c * 128:(c + 1) * 128], mn[mt].ap[:, c * 128:(c + 1) * 128], identb[:, :])
                return ins
            P.add("pe", tr, reads=mn[mt].hs + [h_identb], writes=[hbk[b]])
            P.add("act", lambda e, mt=mt, bv=bv: e.activation(
                mT.ap[:, :, mt * 128:(mt + 1) * 128], bv.rearrange("p (c j) -> p c j", c=KC), AF.Copy),
                reads=[hbk[b]], writes=mT.hs)
        for fp in range(4):
            b = nextbank()

            def mmk(e, fp=fp, b=b):
                ins = None
                for j in range(2):
                    fc = fp * 2 + j
                    for k in range(KC):
                        ins = e.matmul(banks[b][:, j * 256:(j + 1) * 256], wk_t[:, k, fc * 128:(fc + 1) * 128],
                                       mT.ap[:, k, :], start=(k == 0), stop=(k == KC - 1))
                return ins
            P.add("pe", mmk, reads=mT.hs + [wk_h], writes=[hbk[b]])
            P.add("act", lambda e, fp=fp, b=b: e.activation(
                xkT.ap[:, fp * 2:fp * 2 + 2, :], banks[b][:, :].rearrange("p (a m) -> p a m", a=2), AF.Copy),
                reads=[hbk[b]], writes=xkT.hs)
        for mt in range(2):
            for nh in range(2):
                b = nextbank()

                def mmv(e, mt=mt, nh=nh, b=b):
                    ins = None
                    for k in range(KC):
                        ins = e.matmul(banks[b][:, :], mT.ap[:, k, mt * 128:(mt + 1) * 128],
                                       wv_t[:, k, nh * 512:(nh + 1) * 512], start=(k == 0), stop=(k == KC - 1))
                    return ins
                P.add("pe", mmv, reads=mT.hs + [wv_h], writes=[hbk[b]])
                P.add("dve", lambda e, mt=mt, nh=nh, b=b: e.tensor_copy(
                    xv.ap[:, mt, nh * 512:(nh + 1) * 512], banks[b][:, :]), reads=[hbk[b]], writes=xv.hs)
        dbg("mT", mT, BF16)
        dbg("xkT", xkT, BF16)
        dbg("xv", xv, BF16)
        dbg("hT", hT_all, BF16)
        wq_t, wq_h = wload(Wd["w_xq"].rearrange("(k p) f -> p k f", p=128), KC, D)
        wo_t, wo_h = wload(Wd["w_xo"].rearrange("(k p) f -> p k f", p=128), KC, D)
        dbg("wo", View(wo_t, [wo_h]), BF16)
        dbg("wq", View(wq_t, [wq_h]), BF16)
        pT = [R3.view(i * 2048, BF16, 2, TB) for i in range(2)]
        rinv = [R3.view(4096 + i * 2048, F32, TB) for i in range(2)]
        xq = [[R3.view(24 * 1024 + i * 8192 + fc * 1024, BF16, TB) for fc in range(KC)] for i in range(2)]
        xo = [[R3.view(40 * 1024 + i * 8192 + fc * 1024, BF16, TB) for fc in range(KC)] for i in range(2)]
        def proj(tb, fc):
            xq_t = xq[tb % 2]
            hrd = []
            for k in range(KC):
                hrd += hT[(k, tb)].hs
            b = nextbank()

            def mmq(e, fc=fc, b=b, tb=tb):
                ins = None
                for k in range(KC):
                    ins = e.matmul(banks[b][:, :], wq_t[:, k, fc * 128:(fc + 1) * 128], hT[(k, tb)].ap,
                                   start=(k == 0), stop=(k == KC - 1))
                return ins
            P.add("pe", mmq, reads=hrd + [wq_h], writes=[hbk[b]])
            P.add("act", lambda e, fc=fc, b=b, xq_t=xq_t: e.activation(xq_t[fc].ap, banks[b][:, :], AF.Copy),
                  reads=[hbk[b]], writes=xq_t[fc].hs)

        def X0(tb, hh):
            xq_t = xq[tb % 2]
            p_t = pT[(tb * XH + hh) % 2]
            bsc = [nextbank(), nextbank()]
            for mc in range(2):
                def mms(e, mc=mc, hh=hh, b=bsc[mc], xq_t=xq_t):
                    ins = None
                    for dc in range(2):
                        fc = hh * 2 + dc
                        ins = e.matmul(banks[b][:, :], xkT.ap[:, fc, mc * 128:(mc + 1) * 128], xq_t[fc].ap,
                                       start=(dc == 0), stop=(dc == 1))
                    return ins
                P.add("pe", mms, reads=xkT.hs + xq_t[hh * 2].hs + xq_t[hh * 2 + 1].hs, writes=[hbk[bsc[mc]]])
                P.add("act", lambda e, mc=mc, b=bsc[mc], p_t=p_t: e.activation(
                    p_t.ap[:, mc, :], banks[b][:, :], AF.Exp, scale=scale), reads=[hbk[bsc[mc]]], writes=p_t.hs)

        def X1(tb, hh):
            xo_t = xo[tb % 2]
            p_t = pT[(tb * XH + hh) % 2]
            r_t = rinv[(tb * XH + hh) % 2]
            bsum = nextbank()

            def mmsum(e, b=bsum, p_t=p_t):
                ins = None
                for mc in range(2):
                    ins = e.matmul(banks[b][:, :], onesb[:, :], p_t.ap[:, mc, :], start=(mc == 0), stop=(mc == 1))
                return ins
            P.add("pe", mmsum, reads=p_t.hs + [h_ones], writes=[hbk[bsum]])
            P.add("act", lambda e, b=bsum: e.activation(banks[b][:, :], banks[b][:, :], AF.Ln),
                  reads=[hbk[bsum]], writes=[hbk[bsum]])
            P.add("act", lambda e, b=bsum, r_t=r_t: e.activation(r_t.ap, banks[b][:, :], AF.Exp, scale=-1.0),
                  reads=[hbk[bsum]], writes=r_t.hs)
            for dc in range(2):
                fc = hh * 2 + dc
                b = nextbank()

                def mmo(e, fc=fc, b=b, p_t=p_t):
                    ins = None
                    for mc in range(2):
                        ins = e.matmul(banks[b][:, :], xv.ap[:, mc, fc * 128:(fc + 1) * 128], p_t.ap[:, mc, :],
                                       start=(mc == 0), stop=(mc == 1))
                    return ins
                P.add("pe", mmo, reads=xv.hs + p_t.hs, writes=[hbk[b]])
                P.add("dve", lambda e, fc=fc, b=b, r_t=r_t, xo_t=xo_t: e.tensor_tensor(
                    xo_t[fc].ap, banks[b][:, :], r_t.ap, ALU.mult), reads=[hbk[b]] + r_t.hs, writes=xo_t[fc].hs)

        for fc in range(KC):
            proj(0, fc)
        for tb in range(NTB):
            xq_t = xq[tb % 2]
            xo_t = xo[tb % 2]
            X0(tb, 0)
            for hh in range(XH):
                if tb + 1 < NTB:
                    proj(tb + 1, 2 * hh)
                    proj(tb + 1, 2 * hh + 1)
                if hh + 1 < XH:
                    X0(tb, hh + 1)
                X1(tb, hh)
            xoh = []
            for fc in range(KC):
                xoh += xo_t[fc].hs
            for oc in range(KC):
                b = nextbank()

                def mmo2(e, oc=oc, b=b, xo_t=xo_t):
                    ins = None
                    for k in range(KC):
                        ins = e.matmul(banks[b][:, :], wo_t[:, k, oc * 128:(oc + 1) * 128], xo_t[k].ap,
                                       start=(k == 0), stop=(k == KC - 1))
                    return ins
                P.add("pe", mmo2, reads=xoh + [wo_h], writes=[hbk[b]])
                if stage == 31:
                    continue
                P.add("dve", lambda e, oc=oc, b=b, tb=tb: e.scalar_tensor_tensor(
                    xT[(oc, tb)].ap, banks[b][:, :], 1.0, xT[(oc, tb)].ap, ALU.mult, ALU.add),
                    reads=[hbk[b]] + xT[(oc, tb)].hs, writes=xT[(oc, tb)].hs)

    DK = 128
    LNG = [math.log(1.0 - 2.0 ** (-5.0 - h)) for h in range(4)]
    CDEC = [math.exp(128.0 * LNG[h]) for h in range(4)]
    TWO_PI = 2.0 * math.pi
    CW1 = 6.28125
    CW2 = TWO_PI - CW1
    PI_LO = 3.1415925
    AT = [R1.view(ec * 4096, BF16, T) for ec in range(KC)]
    BT = [R1.view(32 * 1024 + cc * 4096, BF16, T) for cc in range(KC)]
    h_dec, h_qdec, h_tmpc, h_wgate = H("decT"), H("qdec"), H("tmpc"), H("wgate")

    def mixer_consts():
        P.add("dve", lambda e: e.memset(small[:, 1:2], 1.0), writes=[h_small])
        P.add("dve", lambda e: e.memset(small[:, 2:3], -0.5), writes=[h_small])
        P.add("dve", lambda e: e.memset(small[:, 3:4], 10000.0), writes=[h_small])
        P.add("dve", lambda e: e.tensor_copy(io_f[:, :], io_i[:, :]), reads=[h_io], writes=[h_iof])
        P.add("dve", lambda e: e.tensor_scalar(tmpc[:, :], io_f[:, :], 0.0, None, ALU.is_ge), reads=[h_iof], writes=[h_tmpc])
        for h in range(4):
            P.add("act", lambda e, h=h: e.activation(decT[:, h * 128:(h + 1) * 128], io_f[:, :], AF.Exp, scale=LNG[h]),
                  reads=[h_iof], writes=[h_dec])
            P.add("dve", lambda e, h=h: e.scalar_tensor_tensor(
                decT[:, h * 128:(h + 1) * 128], decT[:, h * 128:(h + 1) * 128], DK ** -0.5, tmpc[:, :], ALU.mult, ALU.mult),
                reads=[h_dec, h_tmpc], writes=[h_dec])
        P.add("pool", lambda e: e.iota(io_i[:, :], [[1, 128]], base=1, channel_multiplier=0), reads=[], writes=[h_io])
        P.add("dve", lambda e: e.tensor_copy(io_f[:, :], io_i[:, :]), reads=[h_io], writes=[h_iof])
        for h in range(4):
            P.add("act", lambda e, h=h: e.activation(qdec[:, h * 128:(h + 1) * 128], io_f[:, :], AF.Exp, scale=LNG[h]),
                  reads=[h_iof], writes=[h_qdec])
        P.add("pool", lambda e: e.iota(io_i[:, 0:1], [[0, 1]], base=127, channel_multiplier=-1), writes=[h_io])
        P.add("dve", lambda e: e.tensor_copy(small[:, 14:15], io_i[:, 0:1]), reads=[h_io], writes=[h_small])
        for h in range(4):
            P.add("act", lambda e, h=h: e.activation(small[:, 8 + h:9 + h], small[:, 14:15], AF.Exp, scale=LNG[h]),
                  reads=[h_small], writes=[h_small])
        P.add("dve", lambda e: e.tensor_scalar(small[:, 8:12], small[:, 8:12], DK ** -0.5, None, ALU.mult),
              reads=[h_small], writes=[h_small])
        P.add("pool", lambda e: e.iota(io_i[:, 0:1], [[0, 1]], base=0, channel_multiplier=1), writes=[h_io])
        P.add("dve", lambda e: e.tensor_copy(small[:, 15:16], io_i[:, 0:1]), reads=[h_io], writes=[h_small])
        P.add("dve", lambda e: e.tensor_scalar(small[:, 12:13], small[:, 15:16], 64.0, None, ALU.is_ge),
              reads=[h_small], writes=[h_small])
        P.add("dve", lambda e: e.scalar_tensor_tensor(small[:, 13:14], small[:, 12:13], -64.0, small[:, 15:16],
                                                      ALU.mult, ALU.add), reads=[h_small], writes=[h_small])
        P.add("dve", lambda e: e.tensor_scalar(small[:, 13:14], small[:, 13:14], -1.0 / 64.0, None, ALU.mult),
              reads=[h_small], writes=[h_small])
        P.add("pool", lambda e: e.tensor_tensor(small[:, 5:6], small[:, 3:4], small[:, 13:14], ALU.pow),
              reads=[h_small], writes=[h_small])
        P.add("dve", lambda e: e.tensor_scalar(small[:, 12:13], small[:, 12:13], 2.0, -1.0, ALU.mult, ALU.add),
              reads=[h_small], writes=[h_small])
        l0 = PCOL["lru_lambda"]
        P.add("act", lambda e: e.activation(small[:, 32:40], pc[:, l0:l0 + 8], AF.Exp, scale=-1.0),
              reads=[h_pc], writes=[h_small])
        P.add("act", lambda e: e.activation(small[:, 40:48], small[:, 32:40], AF.Ln, bias=small[:, 1:2]),
              reads=[h_small], writes=[h_small])
        P.add("dve", lambda e: e.tensor_scalar(small[:, 48:56], small[:, 40:48], -8.0, None, ALU.mult),
              reads=[h_small], writes=[h_small])
        P.add("dve", lambda e: e.tensor_scalar(small[:, 56:64], small[:, 40:48], -16.0, None, ALU.mult),
              reads=[h_small], writes=[h_small])
        P.add("pool", lambda e: e.dma_start(out=wgate[:, 0:1024].rearrange("p (g j) -> p g j", g=8),
                                            in_=Wd["w_rgate"].rearrange("g i j -> i g j")),
              writes=[h_wgate], dma=True)
        P.add("pool", lambda e: e.dma_start(out=wgate[:, 1024:2048].rearrange("p (g j) -> p g j", g=8),
                                            in_=Wd["w_igate"].rearrange("g i j -> i g j")),
              writes=[h_wgate], dma=True)

    w_in_d = Wd["w_in"].rearrange("(k p) f -> p k f", p=128)

    def lru_branch():
        TH = T // 2
        SET = 27 * 1024

        def mk(s):
            o = s * SET
            d = {}
            d["xl"] = R3.view(o, F32, 1028)
            d["xc"] = R3.view(o + 5 * 1024, F32, TH)
            d["hl"] = R3.view(o + 5 * 1024, F32, TH)
            d["ra"] = R3.view(o + 9 * 1024, F32, TH)
            d["ii"] = R3.view(o + 13 * 1024, F32, TH)
            d["tt"] = R3.view(o + 17 * 1024, F32, TH)
            d["gg"] = R3.view(o + 21 * 1024, F32, TH)
            d["xcb"] = R3.view(o + 25 * 1024, BF16, TH)
            return d
        sets = [mk(0), mk(1)]
        hcar = H("lru_carry")
        P.add("dve", lambda e: e.memset(sets[0]["xl"].ap[:, 0:4], 0.0), writes=sets[0]["xl"].hs)

        def ltile(cc):
            return wload_pair(w_in_d[:, :, 3072 + cc * 128:3072 + (cc + 1) * 128],
                              w_in_d[:, :, 4096 + cc * 128:4096 + (cc + 1) * 128], KC, 128)
        tiles = {0: ltile(0)}

        def A(u):
            cc, hf = divmod(u, 2)
            S = sets[hf]
            xl, xc, ra, ii, gg, xcb = S["xl"], S["xc"], S["ra"], S["ii"], S["gg"], S["xcb"]
            wl_t, wl_h = tiles[cc]
            if hf == 0 and cc + 1 < KC:
                tiles[cc + 1] = ltile(cc + 1)
            tbs = [2 * hf, 2 * hf + 1]
            hrd = []
            for k in range(KC):
                for tb in tbs:
                    hrd += hT[(k, tb)].hs
            for r in range(2):
                bb = [nextbank() for _ in range(2)]

                def mm(e, bb=bb, r=r, wl_t=wl_t, tbs=tbs):
                    ins = None
                    for k in range(KC):
                        for j in range(2):
                            ins = e.matmul(banks[bb[j]][:, :], wl_t[:, k, r, :], hT[(k, tbs[j])].ap,
                                           start=(k == 0), stop=(k == KC - 1))
                    return ins
                P.add("pe", mm, reads=hrd + [wl_h], writes=[hbk[i] for i in bb])
                for j in range(2):
                    if r == 0:
                        P.add("act", lambda e, j=j, bb=bb, xl=xl: e.activation(
                            xl.ap[:, 3 + j * TB:3 + (j + 1) * TB], banks[bb[j]][:, :], AF.Copy),
                            reads=[hbk[bb[j]]], writes=xl.hs)
                    else:
                        P.add("act", lambda e, j=j, bb=bb, gg=gg: e.activation(
                            gg.ap[:, j * TB:(j + 1) * TB], banks[bb[j]][:, :], AF.Gelu),
                            reads=[hbk[bb[j]]], writes=gg.hs)
            if hf == 1:
                x0 = sets[0]["xl"]
                P.add("act", lambda e, x0=x0, xl=xl: e.activation(xl.ap[:, 0:3], x0.ap[:, TH:TH + 3], AF.Copy),
                      reads=x0.hs, writes=xl.hs)
            P.add("act", lambda e, cc=cc, xc=xc, xl=xl: e.activation(xc.ap, xl.ap[:, 3:3 + TH], AF.Identity,
                                                                    bias=pcol("conv_b", cc), scale=pcol("conv_w", 24 + cc)),
                  reads=xl.hs + [h_pc], writes=xc.hs)
            for s in range(3):
                P.add("dve", lambda e, cc=cc, s=s, xc=xc, xl=xl: e.scalar_tensor_tensor(
                    xc.ap, xl.ap[:, s:s + TH], pcol("conv_w", s * 8 + cc), xc.ap, ALU.mult, ALU.add),
                    reads=xl.hs + xc.hs + [h_pc], writes=xc.hs)
            P.add("act", lambda e, xc=xc, xcb=xcb: e.activation(xcb.ap, xc.ap, AF.Copy), reads=xc.hs, writes=xcb.hs)
            for gi in range(2):
                bb = [nextbank() for _ in range(2)]

                def mmg(e, bb=bb, gi=gi, cc=cc, xcb=xcb):
                    ins = None
                    for j in range(2):
                        ins = e.matmul(banks[bb[j]][:, :], wgate[:, gi * 1024 + cc * 128:gi * 1024 + (cc + 1) * 128],
                                       xcb.ap[:, j * TB:(j + 1) * TB], start=True, stop=True)
                    return ins
                P.add("pe", mmg, reads=xcb.hs + [h_wgate], writes=[hbk[i] for i in bb])
                dst = ra if gi == 0 else ii
                bname = "b_rgate" if gi == 0 else "b_igate"
                for j in range(2):
                    P.add("act", lambda e, j=j, bb=bb, dst=dst, bname=bname, cc=cc: e.activation(
                        dst.ap[:, j * TB:(j + 1) * TB], banks[bb[j]][:, :], AF.Sigmoid, bias=pcol(bname, cc)),
                        reads=[hbk[bb[j]], h_pc], writes=dst.hs)

        def Bk_(u):
            cc, hf = divmod(u, 2)
            S = sets[hf]
            xc, hl, ra, ii, tt_, gg = S["xc"], S["hl"], S["ra"], S["ii"], S["tt"], S["gg"]
            P.add("pool", lambda e: e.tensor_tensor(ii.ap, ii.ap, xc.ap, ALU.mult), reads=ii.hs + xc.hs, writes=ii.hs)
            P.add("act", lambda e: e.activation(tt_.ap, ra.ap, AF.Exp, scale=small[:, 56 + cc:57 + cc]),
                  reads=ra.hs + [h_small], writes=tt_.hs)
            P.add("act", lambda e: e.activation(ra.ap, ra.ap, AF.Exp, scale=small[:, 48 + cc:49 + cc]),
                  reads=ra.hs + [h_small], writes=ra.hs)
            P.add("act", lambda e: e.activation(tt_.ap, tt_.ap, AF.Ln, bias=small[:, 1:2], scale=-1.0),
                  reads=tt_.hs + [h_small], writes=tt_.hs)
            P.add("act", lambda e: e.activation(tt_.ap, tt_.ap, AF.Exp, scale=0.5), reads=tt_.hs, writes=tt_.hs)
            P.add("dve", lambda e: e.scalar_tensor_tensor(tt_.ap, ii.ap, 1.0, tt_.ap, ALU.mult, ALU.mult),
                  reads=tt_.hs + ii.hs, writes=tt_.hs)
            if hf == 0:
                P.add("dve", lambda e: e.tensor_tensor_scan(hl.ap, ra.ap, tt_.ap, 0.0, ALU.mult, ALU.add),
                      reads=ra.hs + tt_.hs + xc.hs, writes=hl.hs)
                P.add("dve", lambda e: e.tensor_copy(small[:, 24:25], hl.ap[:, TH - 1:TH]), reads=hl.hs, writes=[hcar])
            else:
                P.add("dve", lambda e: e.tensor_tensor_scan(hl.ap, ra.ap, tt_.ap, small[:, 24:25], ALU.mult, ALU.add),
                      reads=ra.hs + tt_.hs + xc.hs + [hcar], writes=hl.hs)
            P.add("pool", lambda e: e.tensor_tensor(BT[cc].ap[:, hf * TH:(hf + 1) * TH], hl.ap, gg.ap, ALU.mult),
                  reads=hl.hs + gg.hs, writes=BT[cc].hs)

        NU = 2 * KC
        A(0)
        for u in range(NU):
            if u + 1 < NU:
                A(u + 1)
            Bk_(u)

    def rope_tables():
        A = R3.view(0, F32, T)
        B = R3.view(8192, F32, T)
        Bi = R3.view(8192, I32, T)
        C = R3.view(16384, F32, T)
        invf = small[:, 5:6]

        def wrap(gt):
            if gt:
                P.add("dve", lambda e: e.tensor_scalar(C.ap, A.ap, math.pi, -TWO_PI, ALU.is_gt, ALU.mult),
                      reads=A.hs, writes=C.hs)
            else:
                P.add("dve", lambda e: e.tensor_scalar(C.ap, A.ap, -math.pi, TWO_PI, ALU.is_lt, ALU.mult),
                      reads=A.hs, writes=C.hs)
            P.add("dve", lambda e: e.scalar_tensor_tensor(A.ap, C.ap, 1.0, A.ap, ALU.mult, ALU.add),
                  reads=A.hs + C.hs, writes=A.hs)

        def clamp():
            P.add("dve", lambda e: e.tensor_scalar(A.ap, A.ap, PI_LO, -PI_LO, ALU.min, ALU.max), reads=A.hs, writes=A.hs)
        P.add("pool", lambda e: e.iota(Bi.ap, [[1, T]], base=0, channel_multiplier=0), writes=Bi.hs)
        P.add("dve", lambda e: e.tensor_copy(A.ap, Bi.ap), reads=Bi.hs, writes=A.hs)
        P.add("dve", lambda e: e.tensor_scalar(A.ap, A.ap, invf, None, ALU.mult), reads=A.hs + [h_small], writes=A.hs)
        P.add("dve", lambda e: e.tensor_scalar(Bi.ap, A.ap, 1.0 / TWO_PI, None, ALU.mult), reads=A.hs, writes=Bi.hs)
        P.add("dve", lambda e: e.tensor_copy(B.ap, Bi.ap), reads=Bi.hs, writes=B.hs)
        P.add("dve", lambda e: e.scalar_tensor_tensor(A.ap, B.ap, -CW1, A.ap, ALU.mult, ALU.add), reads=A.hs + B.hs, writes=A.hs)
        P.add("dve", lambda e: e.scalar_tensor_tensor(A.ap, B.ap, -CW2, A.ap, ALU.mult, ALU.add), reads=A.hs + B.hs, writes=A.hs)
        wrap(True)
        wrap(False)
        clamp()
        P.add("act", lambda e: e.activation(B.ap, A.ap, AF.Sin), reads=A.hs, writes=B.hs)
        P.add("dve", lambda e: e.tensor_scalar(A.ap, A.ap, math.pi / 2.0, None, ALU.add), reads=A.hs, writes=A.hs)
        wrap(True)
        clamp()
        P.add("act", lambda e: e.activation(A.ap, A.ap, AF.Sin), reads=A.hs, writes=A.hs)
        P.add("dve", lambda e: e.tensor_scalar(B.ap, B.ap, small[:, 12:13], None, ALU.mult),
              reads=B.hs + [h_small], writes=B.hs)
        return A, B

    def retention_branch():
        cosT, sinT = rope_tables()
        if stage == 25:
            dbg("cos", cosT, F32)
            dbg("sin", sinT, F32)
            return
        t1 = R3.view(16 * 1024, F32, TB)
        t2 = R3.view(18 * 1024, F32, TB)
        t1k = R3.view(20 * 1024, F32, TB)
        t2k = R3.view(22 * 1024, F32, TB)
        qb = R3.view(24 * 1024, BF16, T)
        qdb = R3.view(28 * 1024, BF16, T)
        kb = R3.view(32 * 1024, BF16, T)
        v_sb = [R3.view(36 * 1024 + i * 512, BF16, 256) for i in range(4)]
        G2 = [R3.view(38 * 1024 + i * 1024, F32, 256) for i in range(4)]
        sg = [R3.view(42 * 1024 + i * 1024, F32, 256) for i in range(2)]
        ret_sb = [R3.view(44 * 1024 + i * 1024, F32, 256) for i in range(4)]
        A_tok = [R3.view(48 * 1024 + i * 512, BF16, 256) for i in range(2)]
        sc_sb = [R3.view(49 * 1024 + i * 256, BF16, 128) for i in range(4)]
        state = R3.view(50 * 1024, F32, 256)
        state_bf = [R3.view(51 * 1024 + i * 512, BF16, 256) for i in range(4)]
        ktok = [R3.view(53 * 1024 + i * 256, BF16, 128) for i in range(2)]
        junk = R3.view(53 * 1024 + 512, BF16, 256)
        hst = [H("st%d" % i) for i in range(4)]
        P.add("sp", lambda e: e.dma_start(out=bcrow[:, :], in_=bass.AP(Wd["ret_gn"].tensor, 0, [[0, 128], [1, D]])),
              writes=[h_bcrow], dma=True)

        def hload(h):
            c = h * 128
            wq = wload(w_in_d[:, :, c:c + 128], KC, 128)
            wqs_ap, wqs_h, ev = walloc(KC * 128)
            wqs = wqs_ap.rearrange("p (k f) -> p k f", k=KC)
            P.add("pool", lambda e: e.dma_start(out=wqs[:, :, 0:64], in_=w_in_d[:, :, c + 64:c + 128]),
                  writes=[wqs_h] + ev, dma=True)
            P.add("pool", lambda e: e.dma_start(out=wqs[:, :, 64:128], in_=w_in_d[:, :, c:c + 64]),
                  writes=[wqs_h], dma=True)
            wk = wload(w_in_d[:, :, 512 + c:512 + c + 128], KC, 128)
            wks_ap, wks_h, ev2 = walloc(KC * 128)
            wks = wks_ap.rearrange("p (k f) -> p k f", k=KC)
            P.add("pool", lambda e: e.dma_start(out=wks[:, :, 0:64], in_=w_in_d[:, :, 512 + c + 64:512 + c + 128]),
                  writes=[wks_h] + ev2, dma=True)
            P.add("pool", lambda e: e.dma_start(out=wks[:, :, 64:128], in_=w_in_d[:, :, 512 + c:512 + c + 64]),
                  writes=[wks_h], dma=True)
            wvg = wload_pair(w_in_d[:, :, 1024 + h * 256:1024 + (h + 1) * 256],
                             w_in_d[:, :, 2048 + h * 256:2048 + (h + 1) * 256], KC, 256)
            return wq, (wqs, wqs_h), wk, (wks, wks_h), wvg

        nxt = hload(0)
        for h in range(4):
            (wq_t, wq_h), (wqs_t, wqs_h), (wk_t, wk_h), (wks_t, wks_h), (wvg_t, wvg_h) = nxt
            for tb in range(NTB):
                hrd = []
                for k in range(KC):
                    hrd += hT[(k, tb)].hs
                bq = []
                for (wt_, wh_) in ((wq_t, wq_h), (wqs_t, wqs_h), (wk_t, wk_h), (wks_t, wks_h)):
                    b = nextbank()
                    bq.append(b)

                    def mm(e, b=b, wt_=wt_, tb=tb):
                        ins = None
                        for k in range(KC):
                            ins = e.matmul(banks[b][:, :], wt_[:, k, :], hT[(k, tb)].ap, start=(k == 0), stop=(k == KC - 1))
                        return ins
                    P.add("pe", mm, reads=hrd + [wh_], writes=[hbk[b]])
                cs = cosT.ap[:, tb * TB:(tb + 1) * TB]
                sn = sinT.ap[:, tb * TB:(tb + 1) * TB]
                P.add("dve", lambda e, b=bq[0], cs=cs: e.tensor_tensor(t1.ap, banks[b][:, :], cs, ALU.mult),
                      reads=[hbk[bq[0]]] + cosT.hs, writes=t1.hs)
                P.add("dve", lambda e, b=bq[1], sn=sn: e.tensor_tensor(t2.ap, banks[b][:, :], sn, ALU.mult),
                      reads=[hbk[bq[1]]] + sinT.hs, writes=t2.hs)
                P.add("dve", lambda e, b=bq[2], cs=cs: e.tensor_tensor(t1k.ap, banks[b][:, :], cs, ALU.mult),
                      reads=[hbk[bq[2]]] + cosT.hs, writes=t1k.hs)
                P.add("dve", lambda e, b=bq[3], sn=sn: e.tensor_tensor(t2k.ap, banks[b][:, :], sn, ALU.mult),
                      reads=[hbk[bq[3]]] + sinT.hs, writes=t2k.hs)
                P.add("pool", lambda e: e.tensor_tensor(t1.ap, t1.ap, t2.ap, ALU.add), reads=t1.hs + t2.hs, writes=t1.hs)
                P.add("pool", lambda e, tb=tb: e.tensor_copy(qb.ap[:, tb * TB:(tb + 1) * TB], t1.ap),
                      reads=t1.hs, writes=qb.hs)
                qd_in1 = bass.AP(qdec, h * 128, [[512, 128], [0, 4], [1, 128]])
                P.add("pool", lambda e, tb=tb, qd_in1=qd_in1: e.tensor_tensor(
                    qdb.ap[:, tb * TB:(tb + 1) * TB].rearrange("p (a i) -> p a i", a=4),
                    t1.ap.rearrange("p (a i) -> p a i", a=4), qd_in1, ALU.mult),
                    reads=t1.hs + [h_qdec], writes=qdb.hs)
                P.add("pool", lambda e, tb=tb: e.tensor_tensor(kb.ap[:, tb * TB:(tb + 1) * TB], t1k.ap, t2k.ap, ALU.add),
                      reads=t1k.hs + t2k.hs, writes=kb.hs)
            if h + 1 < 4:
                nxt = hload(h + 1)
            P.add("dve", lambda e: e.memset(state.ap, 0.0), writes=state.hs)

            def F(c, h=h, wvg_t=wvg_t, wvg_h=wvg_h):
                cols = slice(c * 128, (c + 1) * 128)
                b = nextbank()

                def mmv(e, b=b):
                    ins = None
                    for k in range(KC):
                        ins = e.matmul(banks[b][:, :], hT_all.ap[:, k, cols], wvg_t[:, k, :, :].rearrange("p r f -> p (r f)"),
                                       start=(k == 0), stop=(k == KC - 1))
                    return ins
                hrd = []
                for k in range(KC):
                    hrd += hT[(k, c // 4)].hs
                P.add("pe", mmv, reads=hrd + [wvg_h], writes=[hbk[b]])
                if F_LEVEL < 2:
                    return
                P.add("act", lambda e, b=b: e.activation(v_sb[c % 4].ap, banks[b][:, 0:256], AF.Copy),
                      reads=[hbk[b]], writes=v_sb[c % 4].hs)
                if F_LEVEL < 2.3:
                    return
                P.add("act", lambda e, b=b: e.activation(sg[c % 2].ap, banks[b][:, 256:512], SG_FUNC),
                      reads=[hbk[b]], writes=sg[c % 2].hs)
                if F_LEVEL < 2.6:
                    return
                P.add("pool", lambda e: e.tensor_tensor(G2[c % 4].ap, sg[c % 2].ap, bcrow[:, h * 256:(h + 1) * 256], ALU.mult),
                      reads=sg[c % 2].hs + [h_bcrow], writes=G2[c % 4].hs)
                if F_LEVEL < 3:
                    return
                b2 = nextbank()
                P.add("pe", lambda e, b2=b2: e.matmul(banks[b2][:, 0:128], kb.ap[:, cols], qb.ap[:, cols], start=True, stop=True),
                      reads=kb.hs + qb.hs, writes=[hbk[b2]])
                P.add("dve", lambda e, b2=b2: e.tensor_tensor(sc_sb[c % 4].ap, banks[b2][:, 0:128],
                                                              decT[:, h * 128:(h + 1) * 128], ALU.mult),
                      reads=[hbk[b2], h_dec], writes=sc_sb[c % 4].hs)
                if F_LEVEL < 4:
                    return
                b3 = nextbank()
                b3v = banks[b3][:, :].bitcast(BF16)
                P.add("pe", lambda e, b3v=b3v: e.transpose(b3v[:, 0:128], kb.ap[:, cols], identb[:, :]),
                      reads=kb.hs + [h_identb], writes=[hbk[b3]])
                P.add("act", lambda e, b3v=b3v: e.activation(ktok[c % 2].ap, b3v[:, 0:128], AF.Identity, scale=small[:, 8 + h:9 + h]),
                      reads=[hbk[b3], h_small], writes=ktok[c % 2].hs)

            def Kst(c, h=h):
                b2 = nextbank()
                P.add("pe", lambda e, b2=b2: e.matmul(banks[b2][:, 0:256], ktok[c % 2].ap, v_sb[c % 4].ap, start=True, stop=True),
                      reads=ktok[c % 2].hs + v_sb[c % 4].hs, writes=[hbk[b2]])
                if c < 15:
                    P.add("dve", lambda e, b2=b2: e.scalar_tensor_tensor(state.ap, state.ap, CDEC[h], banks[b2][:, 0:256],
                                                                         ALU.mult, ALU.add),
                          reads=state.hs + [hbk[b2]], writes=state.hs)
                    P.add("dve", lambda e: e.tensor_copy(state_bf[(c + 1) % 4].ap, state.ap),
                          reads=state.hs, writes=state_bf[(c + 1) % 4].hs)

            def M(c, h=h):
                cols = slice(c * 128, (c + 1) * 128)
                b = nextbank()

                def mmr(e, b=b):
                    ins = e.matmul(banks[b][:, 0:256], sc_sb[c % 4].ap, v_sb[c % 4].ap, start=True, stop=(c == 0))
                    if c > 0:
                        ins = e.matmul(banks[b][:, 0:256], qdb.ap[:, cols], state_bf[c % 4].ap, start=False, stop=True)
                    return ins
                P.add("pe", mmr, reads=sc_sb[c % 4].hs + v_sb[c % 4].hs + qdb.hs + (state_bf[c % 4].hs if c > 0 else []),
                      writes=[hbk[b]])
                s0 = 64 + (c % 4) * 8
                P.add("act", lambda e, b=b: e.activation(ret_sb[c % 4].ap, banks[b][:, 0:256], AF.Identity,
                                                         accum_out=small[:, s0:s0 + 1]),
                      reads=[hbk[b]], writes=ret_sb[c % 4].hs + [hst[c % 4]])
                P.add("act", lambda e, b=b: e.activation(junk.ap, banks[b][:, 0:256], AF.Square,
                                                         accum_out=small[:, s0 + 1:s0 + 2]),
                      reads=[hbk[b]], writes=junk.hs + [hst[c % 4]])

            def B1(c, h=h):
                s0 = 64 + (c % 4) * 8
                hs_ = [hst[c % 4]]

                def col(i):
                    return small[:, s0 + i:s0 + i + 1]
                P.add("dve", lambda e: e.tensor_scalar(col(2), col(0), 1.0 / 256.0, None, ALU.mult), reads=hs_, writes=hs_)
                P.add("dve", lambda e: e.tensor_tensor(col(3), col(2), col(2), ALU.mult), reads=hs_, writes=hs_)
                P.add("dve", lambda e: e.scalar_tensor_tensor(col(4), col(1), 1.0 / 256.0, col(3), ALU.mult, ALU.subtract),
                      reads=hs_, writes=hs_)
                P.add("dve", lambda e: e.tensor_scalar(col(7), col(4), EPS, None, ALU.add), reads=hs_, writes=hs_)
                P.add("pool", lambda e: e.tensor_tensor(col(5), col(7), small[:, 2:3], ALU.pow),
                      reads=hs_ + [h_small], writes=hs_)
                P.add("dve", lambda e: e.scalar_tensor_tensor(col(6), col(2), -1.0, col(5), ALU.mult, ALU.mult),
                      reads=hs_, writes=hs_)

            def B2(c, h=h):
                s0 = 64 + (c % 4) * 8
                hs_ = [hst[c % 4]]

                def col(i):
                    return small[:, s0 + i:s0 + i + 1]
                P.add("act", lambda e: e.activation(ret_sb[c % 4].ap, ret_sb[c % 4].ap, AF.Identity, bias=col(6), scale=col(5)),
                      reads=ret_sb[c % 4].hs + hs_, writes=ret_sb[c % 4].hs)
                P.add("pool", lambda e: e.tensor_tensor(A_tok[c % 2].ap, ret_sb[c % 4].ap, G2[c % 4].ap, ALU.mult),
                      reads=ret_sb[c % 4].hs + G2[c % 4].hs, writes=A_tok[c % 2].hs)

            def B3(c, h=h):
                cols = slice(c * 128, (c + 1) * 128)
                b = nextbank()
                bv = banks[b][:, :].bitcast(BF16)

                def trA(e, bv=bv):
                    ins = None
                    for j in range(2):
                        ins = e.transpose(bv[:, j * 128:(j + 1) * 128], A_tok[c % 2].ap[:, j * 128:(j + 1) * 128], identb[:, :])
                    return ins
                P.add("pe", trA, reads=A_tok[c % 2].hs + [h_identb], writes=[hbk[b]])
                for j in range(2):
                    dstv = AT[2 * h + j]
                    P.add("act", lambda e, bv=bv, dstv=dstv, j=j: e.activation(dstv.ap[:, cols], bv[:, j * 128:(j + 1) * 128], AF.Copy),
                          reads=[hbk[b]], writes=dstv.hs)

            for step in range(16 + 5):
                for si, fn_ in ((5, B3), (4, B2), (3, B1), (2, M)):
                    c = step - si
                    if 0 <= c < 16:
                        fn_(c)
                if step < 16:
                    F(step)
                if 0 <= step - 1 < 16:
                    Kst(step - 1)

    def mixer_tail():
        merged = {(oc, tb): R3.view((oc * T + tb * TB) * 2, BF16, TB) for oc in range(KC) for tb in range(NTB)}
        g_sb = [R3.view(32 * 1024 + i * 2048, F32, TB) for i in range(NTB)]
        m1 = [R3.view(40 * 1024 + i * 2048, F32, TB) for i in range(NTB)]
        g2_sb = [R3.view(48 * 1024 + i * 2048, F32, TB) for i in range(NTB)]
        hTh = allhs(hT)
        ATh = []
        BTh = []
        for i in range(KC):
            ATh += AT[i].hs
            BTh += BT[i].hs
        wro_d = Wd["w_ret_o"].rearrange("(k p) f -> p k f", p=128)
        wlo_d = Wd["w_lru_o"].rearrange("(k p) f -> p k f", p=128)
        wbg_d = Wd["w_branch_gate"].rearrange("(k p) f -> p k f", p=128)

        def tload(oc):
            sl = slice(oc * 128, (oc + 1) * 128)
            sl2 = slice(1024 + oc * 128, 1024 + (oc + 1) * 128)
            return (wload(wro_d[:, :, sl], KC, 128), wload(wbg_d[:, :, sl], KC, 128),
                    wload(wlo_d[:, :, sl], KC, 128), wload(wbg_d[:, :, sl2], KC, 128))
        nxt = tload(0)
        for oc in range(KC):
            (wro, wro_h), (wg1, wg1_h), (wlo, wlo_h), (wg2, wg2_h) = nxt
            if oc + 1 < KC:
                nxt = tload(oc + 1)

            def proj(wt_, wh_, rhs_list, rhs_hs):
                bb = [nextbank() for _ in range(NTB)]

                def mm(e, bb=bb, wt_=wt_, rhs_list=rhs_list):
                    ins = None
                    for k in range(KC):
                        for tb in range(NTB):
                            ins = e.matmul(banks[bb[tb]][:, :], wt_[:, k, :], rhs_list(k, tb), start=(k == 0), stop=(k == KC - 1))
                    return ins
                P.add("pe", mm, reads=rhs_hs + [wh_], writes=[hbk[i] for i in bb])
                return bb
            by = proj(wro, wro_h, lambda k, tb: AT[k].ap[:, tb * TB:(tb + 1) * TB], ATh)
            bg = proj(wg1, wg1_h, lambda k, tb: hT[(k, tb)].ap, hTh)
            for tb in range(NTB):
                P.add("act", lambda e, tb=tb, bg=bg, oc=oc: e.activation(g_sb[tb].ap, banks[bg[tb]][:, :], AF.Sigmoid,
                                                                         bias=pcol("b_branch_gate", oc)),
                      reads=[hbk[bg[tb]], h_pc], writes=g_sb[tb].hs)
                P.add("dve", lambda e, tb=tb, by=by: e.tensor_tensor(m1[tb].ap, banks[by[tb]][:, :], g_sb[tb].ap, ALU.mult),
                      reads=[hbk[by[tb]]] + g_sb[tb].hs, writes=m1[tb].hs)
            by2 = proj(wlo, wlo_h, lambda k, tb: BT[k].ap[:, tb * TB:(tb + 1) * TB], BTh)
            bg2 = proj(wg2, wg2_h, lambda k, tb: hT[(k, tb)].ap, hTh)
            for tb in range(NTB):
                P.add("act", lambda e, tb=tb, bg2=bg2, oc=oc: e.activation(g2_sb[tb].ap, banks[bg2[tb]][:, :], AF.Sigmoid,
                                                                           bias=pcol("b_branch_gate", 8 + oc)),
                      reads=[hbk[bg2[tb]], h_pc], writes=g2_sb[tb].hs)
                P.add("dve", lambda e, tb=tb, by2=by2: e.scalar_tensor_tensor(g2_sb[tb].ap, banks[by2[tb]][:, :], 1.0, g2_sb[tb].ap,
                                                                            ALU.mult, ALU.mult),
                      reads=[hbk[by2[tb]]] + g2_sb[tb].hs, writes=g2_sb[tb].hs)
                P.add("pool", lambda e, tb=tb, oc=oc: e.tensor_tensor(merged[(oc, tb)].ap, m1[tb].ap, g2_sb[tb].ap, ALU.add),
                      reads=m1[tb].hs + g2_sb[tb].hs, writes=merged[(oc, tb)].hs)
        wo_t, wo_h = wload(Wd["w_out"].rearrange("(k p) f -> p k f", p=128), KC, D)
        unpark()
        mh = allhs(merged)
        for oc in range(KC):
            bb = [nextbank() for _ in range(NTB)]

            def mmo(e, bb=bb, oc=oc):
                ins = None
                for k in range(KC):
                    for tb in range(NTB):
                        ins = e.matmul(banks[bb[tb]][:, :], wo_t[:, k, oc * 128:(oc + 1) * 128], merged[(k, tb)].ap,
                                       start=(k == 0), stop=(k == KC - 1))
                return ins
            P.add("pe", mmo, reads=mh + [wo_h], writes=[hbk[i] for i in bb])
            for tb in range(NTB):
                P.add("dve", lambda e, tb=tb, bb=bb, oc=oc: e.scalar_tensor_tensor(
                    xT[(oc, tb)].ap, banks[bb[tb]][:, :], 1.0, xT[(oc, tb)].ap, ALU.mult, ALU.add),
                    reads=[hbk[bb[tb]]] + xT[(oc, tb)].hs, writes=xT[(oc, tb)].hs)

    def unpark():
        for c in range(KC):
            whs = []
            for tb in range(NTB):
                whs += xT[(c, tb)].hs
            P.add("sp", lambda e, c=c: e.dma_start(out=xT_all.ap[:, c, :], in_=park_d[:, c * T:(c + 1) * T]),
                  reads=[hpark], writes=whs, dma=True)

    def mixer():
        if stage not in (21,):
            mixer_consts()
        rmsnorm_T("mix_norm")
        for c in range(KC):
            rhs = []
            for tb in range(NTB):
                rhs += xT[(c, tb)].hs
            P.add("sp", lambda e, c=c: e.dma_start(out=park_d[:, c * T:(c + 1) * T], in_=xT_all.ap[:, c, :]),
                  reads=rhs, writes=[hpark], dma=True)
        if stage in (21, 24):
            unpark()
            return
        if stage not in (23, 25, 26):
            lru_branch()
            for cc in range(KC):
                dbg("BT%d" % cc, BT[cc], BF16)
        if stage == 22:
            unpark()
            return
        retention_branch()
        for ec in range(KC):
            dbg("AT%d" % ec, AT[ec], BF16)
        if stage in (23, 25, 26):
            unpark()
            return
        mixer_tail()

    setup_small()
    setup_consts()
    load_x()
    if stage >= 1:
        ffn("ffn1_w1", "ffn1_w3", "ffn1_w2", "ffn1_norm")
    if stage >= 2 and stage not in (30, 31):
        mixer()
    if stage >= 3 and stage not in (21, 22, 23, 24, 25, 26):
        xattn()
    if stage >= 4 and stage not in (30, 31, 21, 22, 23, 24, 25, 26):
        ffn("ffn2_w1", "ffn2_w3", "ffn2_w2", "ffn2_norm")
    store_out(final_norm=(stage >= 5 and stage not in (30, 31, 21, 22, 23, 24, 25, 26)))
    if DEBUG:
        last = P.ops[-1]
        for op in P.ops:
            if op.dma and op.eng == "sp":
                last.deps.append(op)
        DBG_NAMES[:] = dbg_list
    P.emit(nc)
    return nc


_W_NAMES = [n for n, _ in WSHAPES]


def run(inputs, stage=99, ncores=8, trace=False):
    nc = build(stage)
    x = np.ascontiguousarray(np.asarray(inputs["x"], dtype=np.float32))
    mem = np.ascontiguousarray(np.asarray(inputs["mem"], dtype=np.float32))
    shared = {}
    for n, shp in WSHAPES:
        shared[n] = np.ascontiguousarray(np.asarray(inputs[n], dtype=np.float32).reshape(shp))
    in_maps = []
    for b in range(ncores):
        m = {"x": x[b], "mem": mem[b]}
        m.update(shared)
        in_maps.append(m)
    res = run_bass_kernel_spmd(nc, in_maps, core_ids=list(range(ncores)), trace=trace)
    out = np.stack([np.asarray(r["out"]) for r in res.results], axis=0)
    if DEBUG:
        res.dbg = {n: np.asarray(res.results[0][n]) for n in DBG_NAMES}
    return out.astype(np.float32), res


def kernel(**inputs):
    out, _ = run(inputs, stage=99, ncores=8)
    return out
```

```python
import math
import contextlib
import numpy as np
import concourse.bass as bass
import concourse.mybir as mybir
from concourse.bass_utils import run_bass_kernel_spmd

F32 = mybir.dt.float32
BF16 = mybir.dt.bfloat16
I32 = mybir.dt.int32
AF = mybir.ActivationFunctionType
ALU = mybir.AluOpType

ENGS = ("pe", "act", "dve", "pool", "sp")
NDMASEM = 8


class H:
    __slots__ = ("name", "lw", "lr", "dead")

    def __init__(self, name=""):
        self.name = name
        self.lw = None
        self.lr = {}
        self.dead = False


class Op:
    __slots__ = ("eng", "fn", "dma", "id", "deps", "marked", "idx", "sem", "semval", "prev")

    def __init__(self, eng, fn, dma, id):
        self.eng = eng
        self.fn = fn
        self.dma = dma
        self.id = id
        self.deps = []
        self.marked = False
        self.idx = 0
        self.sem = None
        self.semval = 0
        self.prev = None


class Prog:
    def __init__(self):
        self.ops = []

    def add(self, eng, fn, reads=(), writes=(), dma=False):
        op = Op(eng, fn, dma, len(self.ops))
        deps = {}

        def need(d):
            if d is None:
                return
            if (not d.dma) and (not dma) and d.eng == "pe" and eng == "pe":
                return
            key = ("dma", d.id) if d.dma else d.eng
            if key not in deps or deps[key].id < d.id:
                deps[key] = d

        for h in reads:
            assert not h.dead, h.name
            need(h.lw)
        for h in writes:
            need(h.lw)
            for r in h.lr.values():
                need(r)
        rkey = ("dma", op.id) if dma else eng
        for h in reads:
            h.lr[rkey] = op
        for h in writes:
            h.lw = op
            h.lr = {}
        op.deps = list(deps.values())
        self.ops.append(op)
        return op

    def emit(self, nc):
        ops = self.ops
        for op in ops:
            for d in op.deps:
                d.marked = True
        cnt = {e: 0 for e in ENGS}
        dcnt = {e: [] for e in ENGS}
        for op in ops:
            if op.dma:
                lst = dcnt[op.eng]
                k = len(lst)
                op.idx = k
                op.prev = lst[k - NDMASEM] if k >= NDMASEM else None
                lst.append(op)
            elif op.marked:
                cnt[op.eng] += 1
                op.idx = cnt[op.eng]
        with contextlib.ExitStack() as st:
            esem = {e: st.enter_context(nc.semaphore("s_" + e)) for e in ENGS}
            dsem = {e: [st.enter_context(nc.semaphore("d_%s%d" % (e, i))) for i in range(NDMASEM)]
                    for e in ("sp", "pool")}
            for op in ops:
                if op.dma:
                    op.sem = dsem[op.eng][op.idx % NDMASEM]
                    op.semval = 16 * (op.idx // NDMASEM + 1)

            def run(ename, eng):
                waited = {}

                def wait(sem, val):
                    k = id(sem)
                    if waited.get(k, 0) >= val:
                        return
                    waited[k] = val
                    eng.wait_ge(sem, val)

                for op in ops:
                    if op.eng != ename:
                        continue
                    for d in op.deps:
                        if d.dma:
                            wait(d.sem, d.semval)
                        else:
                            wait(esem[d.eng], d.idx)
                    if op.dma and op.prev is not None:
                        wait(op.prev.sem, op.prev.semval)
                    ins = op.fn(eng) if op.fn is not None else None
                    if ins is None:
                        assert not op.marked and not op.dma
                        continue
                    if op.dma:
                        ins.then_inc(op.sem, 16)
                    elif op.marked:
                        ins.then_inc(esem[ename], 1)

            with nc.Block() as block:
                block.tensor(lambda e: run("pe", e))
                block.scalar(lambda e: run("act", e))
                block.vector(lambda e: run("dve", e))
                block.gpsimd(lambda e: run("pool", e))
                block.sync(lambda e: run("sp", e))


class View:
    __slots__ = ("ap", "hs")

    def __init__(self, ap, hs):
        self.ap = ap
        self.hs = hs


class Region:
    def __init__(self, nc, name, nbytes, gran=1024):
        self.t = nc.alloc_sbuf_tensor(name, [128, nbytes // 4], F32)
        self.h = [H("%s_%d" % (name, i)) for i in range((nbytes + gran - 1) // gran)]
        self.gran = gran
        self.nbytes = nbytes

    def view(self, off, dtype, *shape):
        es = 4 if dtype in (F32, I32) else 2
        n = es * int(np.prod(shape))
        assert off % 4 == 0 and off + n <= self.nbytes, (off, n, self.nbytes)
        ap = self.t[:, off // 4:(off + n) // 4]
        if dtype != F32:
            ap = ap.bitcast(dtype)
        if len(shape) == 2:
            ap = ap.rearrange("p (a b) -> p a b", a=shape[0])
        elif len(shape) == 3:
            ap = ap.rearrange("p (a b c) -> p a b c", a=shape[0], b=shape[1])
        return View(ap, self.h[off // self.gran:(off + n - 1) // self.gran + 1])


T = 2048
D = 1024
KC = 8
TB = 512
NTB = 4
DFF = 2816
EPS = 1e-6
MEM = 256
FFN_GROUPS = [(0, 6), (6, 6), (12, 5), (17, 5)]
RINGC = 18432
R3BYTES = 56 * 1024

WSHAPES = [
    ("ffn1_norm", [D]), ("ffn1_w1", [D, DFF]), ("ffn1_w3", [D, DFF]), ("ffn1_w2", [DFF, D]),
    ("mix_norm", [D]), ("w_in", [D, 5120]), ("ret_gn", [D]), ("w_ret_o", [D, D]),
    ("conv_w", [4, D]), ("conv_b", [D]), ("w_rgate", [8, 128, 128]), ("b_rgate", [D]),
    ("w_igate", [8, 128, 128]), ("b_igate", [D]), ("lru_lambda", [D]), ("w_lru_o", [D, D]),
    ("w_branch_gate", [D, 2 * D]), ("b_branch_gate", [2 * D]), ("w_out", [D, D]),
    ("xattn_norm", [D]), ("mem_norm", [D]), ("w_xq", [D, D]), ("w_xk", [D, D]),
    ("w_xv", [D, D]), ("w_xo", [D, D]), ("ffn2_norm", [D]), ("ffn2_w1", [D, DFF]),
    ("ffn2_w3", [D, DFF]), ("ffn2_w2", [DFF, D]), ("final_norm", [D]),
]

PCOL = {}
_r = 0
for _n, _k in [("ffn1_norm", 8), ("mix_norm", 8), ("xattn_norm", 8), ("ffn2_norm", 8),
               ("conv_w", 32), ("conv_b", 8), ("b_rgate", 8), ("b_igate", 8),
               ("lru_lambda", 8), ("b_branch_gate", 16)]:
    PCOL[_n] = _r
    _r += _k
NPCOL = _r


DEBUG = False
F_LEVEL = 4
SG_FUNC = AF.Silu
RET_LEVEL = 4
DBG_NAMES = []


def build(stage=99):
    nc = bass.Bass("TRN2", target_bir_lowering=False)
    P = Prog()

    def dram(name, shape, kind="ExternalInput"):
        return nc.dram_tensor(name, list(shape), F32, kind=kind).ap()

    x_d = dram("x", [T, D])
    mem_d = dram("mem", [MEM, D])
    Wd = {n: dram(n, s) for n, s in WSHAPES}
    out_d = dram("out", [T, D], kind="ExternalOutput")
    park_d = nc.dram_tensor("park", [128, KC * T], F32).ap()
    hpark = H("park")

    identf = nc.alloc_sbuf_tensor("identf", [128, 128], F32)
    identb = nc.alloc_sbuf_tensor("identb", [128, 128], BF16)
    onesb = nc.alloc_sbuf_tensor("onesb", [128, 128], BF16)
    io_i = nc.alloc_sbuf_tensor("io_i", [128, 128], I32)
    io_f = nc.alloc_sbuf_tensor("io_f", [128, 128], F32)
    prow = nc.alloc_sbuf_tensor("prow", [128, 128], F32)
    pc = nc.alloc_sbuf_tensor("pc", [128, 128], F32)
    bcrow = nc.alloc_sbuf_tensor("bcrow", [128, D], F32)
    small = nc.alloc_sbuf_tensor("small", [128, 256], F32)
    wgate = nc.alloc_sbuf_tensor("wgate", [128, 2048], BF16)
    decT = nc.alloc_sbuf_tensor("decT", [128, 512], F32)
    qdec = nc.alloc_sbuf_tensor("qdec", [128, 512], F32)
    tmpc = nc.alloc_sbuf_tensor("tmpc", [128, 128], F32)
    h_identf, h_identb, h_ones, h_io, h_iof, h_prow, h_pc, h_bcrow = (H(n) for n in
        ("identf", "identb", "ones", "io", "iof", "prow", "pc", "bcrow"))
    ring = nc.alloc_sbuf_tensor("ring", [128, RINGC], BF16)
    R1 = Region(nc, "R1", 64 * 1024)
    R2 = Region(nc, "R2", 32 * 1024)
    R3 = Region(nc, "R3", R3BYTES)
    banks = [nc.alloc_psum_tensor("bank%d" % i, [128, 512], F32) for i in range(8)]
    hbk = [H("bank%d" % i) for i in range(8)]
    bstate = {"i": 0}

    def nextbank():
        i = bstate["i"]
        bstate["i"] = (i + 1) % 8
        return i

    xT = {(c, tb): R1.view((c * T + tb * TB) * 4, F32, TB) for c in range(KC) for tb in range(NTB)}
    hT = {(c, tb): R2.view((c * T + tb * TB) * 2, BF16, TB) for c in range(KC) for tb in range(NTB)}
    xT_all = R1.view(0, F32, KC, T)
    hT_all = R2.view(0, BF16, KC, T)

    rstate = {"ptr": 0, "live": []}

    def walloc(ncols):
        assert ncols <= RINGC
        if rstate["ptr"] + ncols > RINGC:
            rstate["ptr"] = 0
        a = rstate["ptr"]
        b = a + ncols
        ev = []
        keep = []
        for (s, e, h) in rstate["live"]:
            if s < b and a < e:
                h.dead = True
                ev.append(h)
            else:
                keep.append((s, e, h))
        h = H("w@%d" % a)
        keep.append((a, b, h))
        rstate["live"] = keep
        rstate["ptr"] = b
        return ring[:, a:b], h, ev

    def wload(src, *shape):
        n = int(np.prod(shape))
        ap, h, ev = walloc(n)
        if len(shape) == 2:
            dst = ap.rearrange("p (k f) -> p k f", k=shape[0])
        elif len(shape) == 3:
            dst = ap.rearrange("p (k r f) -> p k r f", k=shape[0], r=shape[1])
        else:
            dst = ap
        P.add("pool", lambda e, dst=dst, src=src: e.dma_start(out=dst, in_=src),
              reads=[], writes=[h] + ev, dma=True)
        return dst, h

    def wload_pair(src0, src1, k, x):
        ap, h, ev = walloc(k * 2 * x)
        dst = ap.rearrange("p (k r f) -> p k r f", k=k, r=2)
        P.add("pool", lambda e: e.dma_start(out=dst[:, :, 0, :], in_=src0), reads=[], writes=[h] + ev, dma=True)
        P.add("pool", lambda e: e.dma_start(out=dst[:, :, 1, :], in_=src1), reads=[], writes=[h], dma=True)
        return dst, h

    dbg_list = []

    def dbg(name, view, dtype):
        if not DEBUG:
            return
        shp = [128] + [int(s) for s in view.ap.shape[1:]]
        t = nc.dram_tensor("dbg_" + name, shp, dtype, kind="ExternalOutput").ap()
        hh = H("dbg_" + name)
        P.add("sp", lambda e: e.dma_start(out=t, in_=view.ap), reads=view.hs, writes=[hh], dma=True)
        dbg_list.append("dbg_" + name)

    def setup_consts():
        P.add("pool", lambda e: e.iota(io_i[:, :], [[1, 128]], base=0, channel_multiplier=-1),
              writes=[h_io])
        P.add("dve", lambda e: e.tensor_scalar(identf[:, :], io_i[:, :], 0.0, None, ALU.is_equal),
              reads=[h_io], writes=[h_identf])
        P.add("dve", lambda e: e.tensor_copy(identb[:, :], identf[:, :]), reads=[h_identf], writes=[h_identb])
        P.add("dve", lambda e: e.memset(onesb[:, :], 1.0), writes=[h_ones])
        for n in ("ffn1_norm", "mix_norm", "xattn_norm", "ffn2_norm", "conv_b", "b_rgate", "b_igate",
                  "lru_lambda", "b_branch_gate"):
            r0 = PCOL[n]
            src = Wd[n].rearrange("(c p) -> c p", p=128)
            nr = src.shape[0]
            P.add("sp", lambda e, r0=r0, nr=nr, src=src: e.dma_start(out=prow[r0:r0 + nr, :], in_=src),
                  writes=[h_prow], dma=True)
        r0 = PCOL["conv_w"]
        src = Wd["conv_w"].rearrange("k (c p) -> (k c) p", p=128)
        P.add("sp", lambda e, r0=r0, src=src: e.dma_start(out=prow[r0:r0 + 32, :], in_=src),
              writes=[h_prow], dma=True)
        b = nextbank()
        P.add("pe", lambda e, b=b: e.transpose(banks[b][:, 0:NPCOL], prow[0:NPCOL, :], identf[0:NPCOL, 0:NPCOL]),
              reads=[h_prow, h_identf], writes=[hbk[b]])
        P.add("dve", lambda e, b=b: e.tensor_copy(pc[:, 0:NPCOL], banks[b][:, 0:NPCOL]),
              reads=[hbk[b]], writes=[h_pc])

    def pcol(name, i):
        r = PCOL[name] + i
        return pc[:, r:r + 1]

    def load_x():
        stg = [R3.view(i * 4096, F32, D) for i in range(4)]
        for tt in range(16):
            s = stg[tt % 4]
            P.add("sp", lambda e, s=s, tt=tt: e.dma_start(out=s.ap, in_=x_d[tt * 128:(tt + 1) * 128, :]),
                  writes=s.hs, dma=True)
            for half in range(2):
                b = nextbank()

                def tr(e, s=s, b=b, half=half):
                    ins = None
                    for j in range(4):
                        c = half * 4 + j
                        ins = e.transpose(banks[b][:, j * 128:(j + 1) * 128], s.ap[:, c * 128:(c + 1) * 128],
                                          identf[:, :])
                    return ins
                P.add("pe", tr, reads=s.hs + [h_identf], writes=[hbk[b]])
                dst = xT_all.ap[:, half * 4:half * 4 + 4, tt * 128:(tt + 1) * 128]
                src = banks[b][:, :].rearrange("p (a b) -> p a b", a=4)
                whs = []
                for j in range(4):
                    whs += xT[(half * 4 + j, tt // 4)].hs
                if half == 0:
                    P.add("act", lambda e, dst=dst, src=src: e.activation(dst, src, AF.Copy),
                          reads=[hbk[b]], writes=whs)
                else:
                    P.add("dve", lambda e, dst=dst, src=src: e.tensor_copy(dst, src),
                          reads=[hbk[b]], writes=whs)

    def rmsnorm_T(gname):
        sq = [R3.view(40 * 1024 + c * 1024, BF16, TB) for c in range(KC)]
        for tb in range(NTB):
            for c in range(KC):
                P.add("act", lambda e, c=c, tb=tb: e.activation(sq[c].ap, xT[(c, tb)].ap, AF.Square),
                      reads=xT[(c, tb)].hs, writes=sq[c].hs)
            b = nextbank()

            def mm(e, b=b):
                ins = None
                for c in range(KC):
                    ins = e.matmul(banks[b][:, :], onesb[:, :], sq[c].ap, start=(c == 0), stop=(c == KC - 1))
                return ins
            rhs = []
            for c in range(KC):
                rhs += sq[c].hs
            P.add("pe", mm, reads=rhs + [h_ones], writes=[hbk[b]])
            P.add("act", lambda e, b=b: e.activation(banks[b][:, :], banks[b][:, :], AF.Ln, bias=small[:, 0:1],
                                                     scale=1.0 / D),
                  reads=[hbk[b], h_small], writes=[hbk[b]])
            P.add("act", lambda e, b=b: e.activation(banks[b][:, :], banks[b][:, :], AF.Exp, scale=-0.5),
                  reads=[hbk[b]], writes=[hbk[b]])
            for c in range(KC):
                P.add("dve", lambda e, c=c, tb=tb, b=b: e.scalar_tensor_tensor(
                    hT[(c, tb)].ap, xT[(c, tb)].ap, pcol(gname, c), banks[b][:, :], ALU.mult, ALU.mult),
                    reads=xT[(c, tb)].hs + [hbk[b], h_pc], writes=hT[(c, tb)].hs)

    h_small = H("small")

    def setup_small():
        P.add("dve", lambda e: e.memset(small[:, 0:1], EPS), writes=[h_small])

    def ffn(n1, n3, n2, gname):
        w1_d = Wd[n1].rearrange("(k p) f -> p k f", p=128)
        w3_d = Wd[n3].rearrange("(k p) f -> p k f", p=128)
        w2_d = Wd[n2].rearrange("(c p) f -> p c f", p=128)
        hid = {(m, tb): R3.view((m * T + tb * TB) * 2, BF16, TB) for m in range(6) for tb in range(NTB)}
        ssb = [R3.view(24 * 1024 + i * 2048, F32, TB) for i in range(NTB)]
        wt = {}

        def load13(g):
            c0, G = FFN_GROUPS[g]
            wt[("w1", g)] = wload(w1_d[:, :, c0 * 128:(c0 + G) * 128], KC, G * 128)
            wt[("w3", g)] = wload(w3_d[:, :, c0 * 128:(c0 + G) * 128], KC, G * 128)

        def load2(g):
            c0, G = FFN_GROUPS[g]
            wt[("w2", g)] = wload(w2_d[:, c0:c0 + G, :], G, D)

        load13(0)
        rmsnorm_T(gname)
        allh = []
        for c in range(KC):
            for tb in range(NTB):
                allh += hT[(c, tb)].hs
        for g, (c0, G) in enumerate(FFN_GROUPS):
            load2(g)
            w1t, w1h = wt[("w1", g)]
            w3t, w3h = wt[("w3", g)]
            for m in range(G):
                bs = []
                for wtile, wh in ((w1t, w1h), (w3t, w3h)):
                    bb = [nextbank() for _ in range(NTB)]
                    bs.append(bb)

                    def mm(e, wtile=wtile, bb=bb, m=m):
                        ins = None
                        for k in range(KC):
                            for tb in range(NTB):
                                ins = e.matmul(banks[bb[tb]][:, :], wtile[:, k, m * 128:(m + 1) * 128],
                                               hT[(k, tb)].ap, start=(k == 0), stop=(k == KC - 1))
                        return ins
                    P.add("pe", mm, reads=allh + [wh], writes=[hbk[i] for i in bb])
                ba, bb_ = bs
                for tb in range(NTB):
                    P.add("act", lambda e, tb=tb, ba=ba: e.activation(ssb[tb].ap, banks[ba[tb]][:, :], AF.Silu),
                          reads=[hbk[ba[tb]]], writes=ssb[tb].hs)
                for tb in range(NTB):
                    P.add("dve", lambda e, tb=tb, bb_=bb_, m=m: e.tensor_tensor(
                        hid[(m, tb)].ap, banks[bb_[tb]][:, :], ssb[tb].ap, ALU.mult),
                        reads=[hbk[bb_[tb]]] + ssb[tb].hs, writes=hid[(m, tb)].hs)
            if g + 1 < len(FFN_GROUPS):
                load13(g + 1)
            w2t, w2h = wt[("w2", g)]
            hidh = []
            for m in range(G):
                for tb in range(NTB):
                    hidh += hid[(m, tb)].hs
            for oc in range(KC):
                bb = [nextbank() for _ in range(NTB)]

                def mm2(e, bb=bb, oc=oc, G=G, w2t=w2t):
                    ins = None
                    for m in range(G):
                        for tb in range(NTB):
                            ins = e.matmul(banks[bb[tb]][:, :], w2t[:, m, oc * 128:(oc + 1) * 128],
                                           hid[(m, tb)].ap, start=(m == 0), stop=(m == G - 1))
                    return ins
                P.add("pe", mm2, reads=hidh + [w2h], writes=[hbk[i] for i in bb])
                for tb in range(NTB):
                    P.add("dve", lambda e, tb=tb, bb=bb, oc=oc: e.scalar_tensor_tensor(
                        xT[(oc, tb)].ap, banks[bb[tb]][:, :], 0.5, xT[(oc, tb)].ap, ALU.mult, ALU.add),
                        reads=[hbk[bb[tb]]] + xT[(oc, tb)].hs, writes=xT[(oc, tb)].hs)

    def store_out(final_norm):
        ost = [R3.view(i * 4096, F32, D) for i in range(2)]
        st = [R3.view(8192 + i * 64, F32, 16) for i in range(4)]
        junk = R3.view(8192 + 1024, BF16, TB)
        if final_norm:
            P.add("sp", lambda e: e.dma_start(out=bcrow[:, :], in_=bass.AP(Wd["final_norm"].tensor, 0, [[0, 128], [1, D]])),
                  writes=[h_bcrow], dma=True)
        outh = H("out")
        for tt in range(16):
            o = ost[tt % 2]
            s = st[tt % 4]
            bb = [nextbank(), nextbank()]
            for half in range(2):
                def tr(e, half=half, tt=tt, b=bb[half]):
                    ins = None
                    for j in range(4):
                        c = half * 4 + j
                        ins = e.transpose(banks[b][:, j * 128:(j + 1) * 128],
                                          xT_all.ap[:, c, tt * 128:(tt + 1) * 128], identf[:, :])
                    return ins
                rh = []
                for j in range(4):
                    rh += xT[(half * 4 + j, tt // 4)].hs
                P.add("pe", tr, reads=rh + [h_identf], writes=[hbk[bb[half]]])
            if final_norm:
                for half in range(2):
                    P.add("act", lambda e, half=half, b=bb[half], s=s: e.activation(
                        junk.ap, banks[b][:, :], AF.Square, accum_out=s.ap[:, half:half + 1]),
                        reads=[hbk[bb[half]]], writes=s.hs + junk.hs)
                P.add("dve", lambda e, s=s: e.tensor_tensor(s.ap[:, 2:3], s.ap[:, 0:1], s.ap[:, 1:2], ALU.add),
                      reads=s.hs, writes=s.hs)
                P.add("act", lambda e, s=s: e.activation(s.ap[:, 3:4], s.ap[:, 2:3], AF.Ln, bias=small[:, 0:1],
                                                         scale=1.0 / D), reads=s.hs + [h_small], writes=s.hs)
                P.add("act", lambda e, s=s: e.activation(s.ap[:, 4:5], s.ap[:, 3:4], AF.Exp, scale=-0.5),
                      reads=s.hs, writes=s.hs)
                for half in range(2):
                    P.add("dve", lambda e, half=half, b=bb[half], s=s, o=o: e.scalar_tensor_tensor(
                        o.ap[:, half * 512:(half + 1) * 512], banks[b][:, :], s.ap[:, 4:5],
                        bcrow[:, half * 512:(half + 1) * 512], ALU.mult, ALU.mult),
                        reads=[hbk[bb[half]], h_bcrow] + s.hs, writes=o.hs)
            else:
                P.add("act", lambda e, b=bb[0], o=o: e.activation(o.ap[:, 0:512], banks[b][:, :], AF.Copy),
                      reads=[hbk[bb[0]]], writes=o.hs)
                P.add("dve", lambda e, b=bb[1], o=o: e.tensor_copy(o.ap[:, 512:1024], banks[b][:, :]),
                      reads=[hbk[bb[1]]], writes=o.hs)
            P.add("sp", lambda e, o=o, tt=tt: e.dma_start(out=out_d[tt * 128:(tt + 1) * 128, :], in_=o.ap),
                  reads=o.hs, writes=[outh], dma=True)
        ohs = [H("o%d" % i) for i in range(1)]
        P.add("sp", None, reads=[outh])
        for op in [o for o in P.ops if o.dma and o.eng == "sp"][-16:]:
            P.ops[-1].deps.append(op)

    def allhs(d):
        out = []
        for v in d.values():
            out += v.hs
        return out

    def xattn():
        XH = 4
        scale = 1.0 / 16.0
        wk_t, wk_h = wload(Wd["w_xk"].rearrange("(k p) f -> p k f", p=128), KC, D)
        wv_t, wv_h = wload(Wd["w_xv"].rearrange("(k p) f -> p k f", p=128), KC, D)
        rmsnorm_T("xattn_norm")
        hTh = allhs(hT)
        P.add("sp", lambda e: e.dma_start(out=bcrow[:, :], in_=bass.AP(Wd["mem_norm"].tensor, 0, [[0, 128], [1, D]])),
              writes=[h_bcrow], dma=True)
        mstg = [R3.view(i * 4096, F32, D) for i in range(2)]
        mn = [R3.view(8192 + i * 2048, BF16, D) for i in range(2)]
        mT = R3.view(12 * 1024, BF16, KC, MEM)
        xkT = R3.view(16 * 1024, BF16, KC, MEM)
        xv = R3.view(20 * 1024, BF16, 2, D)
        junk = R3.view(24 * 1024, BF16, D)
        hs_m = H("memstat")
        for mt in range(2):
            s = mstg[mt]
            P.add("sp", lambda e, s=s, mt=mt: e.dma_start(out=s.ap, in_=mem_d[mt * 128:(mt + 1) * 128, :]),
                  writes=s.hs, dma=True)
            c0 = 16 + mt * 4
            P.add("act", lambda e, s=s, c0=c0: e.activation(junk.ap, s.ap, AF.Square, accum_out=small[:, c0:c0 + 1]),
                  reads=s.hs, writes=junk.hs + [hs_m])
            P.add("act", lambda e, c0=c0: e.activation(small[:, c0 + 1:c0 + 2], small[:, c0:c0 + 1], AF.Ln,
                                                       bias=small[:, 0:1], scale=1.0 / D),
                  reads=[hs_m, h_small], writes=[hs_m])
            P.add("act", lambda e, c0=c0: e.activation(small[:, c0 + 2:c0 + 3], small[:, c0 + 1:c0 + 2], AF.Exp, scale=-0.5),
                  reads=[hs_m], writes=[hs_m])
            P.add("dve", lambda e, s=s, c0=c0, mt=mt: e.scalar_tensor_tensor(
                mn[mt].ap, s.ap, small[:, c0 + 2:c0 + 3], bcrow[:, :], ALU.mult, ALU.mult),
                reads=s.hs + [hs_m, h_bcrow], writes=mn[mt].hs)
            b = nextbank()
            bv = banks[b][:, :].bitcast(BF16)

            def tr(e, mt=mt, bv=bv):
                ins = None
                for c in range(KC):
                    ins = e.transpose(bv[:, c * 128:(c + 1) * 128], mn[mt].ap[:, c * 128:(c + 1) * 128], identb[:, :])
                return ins
            P.add("pe", tr, reads=mn[mt].hs + [h_identb], writes=[hbk[b]])
            P.add("act", lambda e, mt=mt, bv=bv: e.activation(
                mT.ap[:, :, mt * 128:(mt + 1) * 128], bv.rearrange("p (c j) -> p c j", c=KC), AF.Copy),
                reads=[hbk[b]], writes=mT.hs)
        for fp in range(4):
            b = nextbank()

            def mmk(e, fp=fp, b=b):
                ins = None
                for j in range(2):
                    fc = fp * 2 + j
                    for k in range(KC):
                        ins = e.matmul(banks[b][:, j * 256:(j + 1) * 256], wk_t[:, k, fc * 128:(fc + 1) * 128],
                                       mT.ap[:, k, :], start=(k == 0), stop=(k == KC - 1))
                return ins
            P.add("pe", mmk, reads=mT.hs + [wk_h], writes=[hbk[b]])
            P.add("act", lambda e, fp=fp, b=b: e.activation(
                xkT.ap[:, fp * 2:fp * 2 + 2, :], banks[b][:, :].rearrange("p (a m) -> p a m", a=2), AF.Copy),
                reads=[hbk[b]], writes=xkT.hs)
        for mt in range(2):
            for nh in range(2):
                b = nextbank()

                def mmv(e, mt=mt, nh=nh, b=b):
                    ins = None
                    for k in range(KC):
                        ins = e.matmul(banks[b][:, :], mT.ap[:, k, mt * 128:(mt + 1) * 128],
                                       wv_t[:, k, nh * 512:(nh + 1) * 512], start=(k == 0), stop=(k == KC - 1))
                    return ins
                P.add("pe", mmv, reads=mT.hs + [wv_h], writes=[hbk[b]])
                P.add("dve", lambda e, mt=mt, nh=nh, b=b: e.tensor_copy(
                    xv.ap[:, mt, nh * 512:(nh + 1) * 512], banks[b][:, :]), reads=[hbk[b]], writes=xv.hs)
        dbg("mT", mT, BF16)
        dbg("xkT", xkT, BF16)
        dbg("xv", xv, BF16)
        dbg("hT", hT_all, BF16)
        wq_t, wq_h = wload(Wd["w_xq"].rearrange("(k p) f -> p k f", p=128), KC, D)
        wo_t, wo_h = wload(Wd["w_xo"].rearrange("(k p) f -> p k f", p=128), KC, D)
        dbg("wo", View(wo_t, [wo_h]), BF16)
        dbg("wq", View(wq_t, [wq_h]), BF16)
        pT = [R3.view(i * 2048, BF16, 2, TB) for i in range(2)]
        rinv = [R3.view(4096 + i * 2048, F32, TB) for i in range(2)]
        xq = [[R3.view(24 * 1024 + i * 8192 + fc * 1024, BF16, TB) for fc in range(KC)] for i in range(2)]
        xo = [[R3.view(40 * 1024 + i * 8192 + fc * 1024, BF16, TB) for fc in range(KC)] for i in range(2)]
        def proj(tb, fc):
            xq_t = xq[tb % 2]
            hrd = []
            for k in range(KC):
                hrd += hT[(k, tb)].hs
            b = nextbank()

            def mmq(e, fc=fc, b=b, tb=tb):
                ins = None
                for k in range(KC):
                    ins = e.matmul(banks[b][:, :], wq_t[:, k, fc * 128:(fc + 1) * 128], hT[(k, tb)].ap,
                                   start=(k == 0), stop=(k == KC - 1))
                return ins
            P.add("pe", mmq, reads=hrd + [wq_h], writes=[hbk[b]])
            P.add("act", lambda e, fc=fc, b=b, xq_t=xq_t: e.activation(xq_t[fc].ap, banks[b][:, :], AF.Copy),
                  reads=[hbk[b]], writes=xq_t[fc].hs)

        def X0(tb, hh):
            xq_t = xq[tb % 2]
            p_t = pT[(tb * XH + hh) % 2]
            bsc = [nextbank(), nextbank()]
            for mc in range(2):
                def mms(e, mc=mc, hh=hh, b=bsc[mc], xq_t=xq_t):
                    ins = None
                    for dc in range(2):
                        fc = hh * 2 + dc
                        ins = e.matmul(banks[b][:, :], xkT.ap[:, fc, mc * 128:(mc + 1) * 128], xq_t[fc].ap,
                                       start=(dc == 0), stop=(dc == 1))
                    return ins
                P.add("pe", mms, reads=xkT.hs + xq_t[hh * 2].hs + xq_t[hh * 2 + 1].hs, writes=[hbk[bsc[mc]]])
                P.add("act", lambda e, mc=mc, b=bsc[mc], p_t=p_t: e.activation(
                    p_t.ap[:, mc, :], banks[b][:, :], AF.Exp, scale=scale), reads=[hbk[bsc[mc]]], writes=p_t.hs)

        def X1(tb, hh):
            xo_t = xo[tb % 2]
            p_t = pT[(tb * XH + hh) % 2]
            r_t = rinv[(tb * XH + hh) % 2]
            bsum = nextbank()

            def mmsum(e, b=bsum, p_t=p_t):
                ins = None
                for mc in range(2):
                    ins = e.matmul(banks[b][:, :], onesb[:, :], p_t.ap[:, mc, :], start=(mc == 0), stop=(mc == 1))
                return ins
            P.add("pe", mmsum, reads=p_t.hs + [h_ones], writes=[hbk[bsum]])
            P.add("act", lambda e, b=bsum: e.activation(banks[b][:, :], banks[b][:, :], AF.Ln),
                  reads=[hbk[bsum]], writes=[hbk[bsum]])
            P.add("act", lambda e, b=bsum, r_t=r_t: e.activation(r_t.ap, banks[b][:, :], AF.Exp, scale=-1.0),
                  reads=[hbk[bsum]], writes=r_t.hs)
            for dc in range(2):
                fc = hh * 2 + dc
                b = nextbank()

                def mmo(e, fc=fc, b=b, p_t=p_t):
                    ins = None
                    for mc in range(2):
                        ins = e.matmul(banks[b][:, :], xv.ap[:, mc, fc * 128:(fc + 1) * 128], p_t.ap[:, mc, :],
                                       start=(mc == 0), stop=(mc == 1))
                    return ins
                P.add("pe", mmo, reads=xv.hs + p_t.hs, writes=[hbk[b]])
                P.add("dve", lambda e, fc=fc, b=b, r_t=r_t, xo_t=xo_t: e.tensor_tensor(
                    xo_t[fc].ap, banks[b][:, :], r_t.ap, ALU.mult), reads=[hbk[b]] + r_t.hs, writes=xo_t[fc].hs)

        for fc in range(KC):
            proj(0, fc)
        for tb in range(NTB):
            xq_t = xq[tb % 2]
            xo_t = xo[tb % 2]
            X0(tb, 0)
            for hh in range(XH):
                if tb + 1 < NTB:
                    proj(tb + 1, 2 * hh)
                    proj(tb + 1, 2 * hh + 1)
                if hh + 1 < XH:
                    X0(tb, hh + 1)
                X1(tb, hh)
            xoh = []
            for fc in range(KC):
                xoh += xo_t[fc].hs
            for oc in range(KC):
                b = nextbank()

                def mmo2(e, oc=oc, b=b, xo_t=xo_t):
                    ins = None
                    for k in range(KC):
                        ins = e.matmul(banks[b][:, :], wo_t[:, k, oc * 128:(oc + 1) * 128], xo_t[k].ap,
                                       start=(k == 0), stop=(k == KC - 1))
                    return ins
                P.add("pe", mmo2, reads=xoh + [wo_h], writes=[hbk[b]])
                if stage == 31:
                    continue
                P.add("dve", lambda e, oc=oc, b=b, tb=tb: e.scalar_tensor_tensor(
                    xT[(oc, tb)].ap, banks[b][:, :], 1.0, xT[(oc, tb)].ap, ALU.mult, ALU.add),
                    reads=[hbk[b]] + xT[(oc, tb)].hs, writes=xT[(oc, tb)].hs)

    DK = 128
    LNG = [math.log(1.0 - 2.0 ** (-5.0 - h)) for h in range(4)]
    CDEC = [math.exp(128.0 * LNG[h]) for h in range(4)]
    TWO_PI = 2.0 * math.pi
    CW1 = 6.28125
    CW2 = TWO_PI - CW1
    PI_LO = 3.1415925
    AT = [R1.view(ec * 4096, BF16, T) for ec in range(KC)]
    BT = [R1.view(32 * 1024 + cc * 4096, BF16, T) for cc in range(KC)]
    h_dec, h_qdec, h_tmpc, h_wgate = H("decT"), H("qdec"), H("tmpc"), H("wgate")

    def mixer_consts():
        P.add("dve", lambda e: e.memset(small[:, 1:2], 1.0), writes=[h_small])
        P.add("dve", lambda e: e.memset(small[:, 2:3], -0.5), writes=[h_small])
        P.add("dve", lambda e: e.memset(small[:, 3:4], 10000.0), writes=[h_small])
        P.add("dve", lambda e: e.tensor_copy(io_f[:, :], io_i[:, :]), reads=[h_io], writes=[h_iof])
        P.add("dve", lambda e: e.tensor_scalar(tmpc[:, :], io_f[:, :], 0.0, None, ALU.is_ge), reads=[h_iof], writes=[h_tmpc])
        for h in range(4):
            P.add("act", lambda e, h=h: e.activation(decT[:, h * 128:(h + 1) * 128], io_f[:, :], AF.Exp, scale=LNG[h]),
                  reads=[h_iof], writes=[h_dec])
            P.add("dve", lambda e, h=h: e.scalar_tensor_tensor(
                decT[:, h * 128:(h + 1) * 128], decT[:, h * 128:(h + 1) * 128], DK ** -0.5, tmpc[:, :], ALU.mult, ALU.mult),
                reads=[h_dec, h_tmpc], writes=[h_dec])
        P.add("pool", lambda e: e.iota(io_i[:, :], [[1, 128]], base=1, channel_multiplier=0), reads=[], writes=[h_io])
        P.add("dve", lambda e: e.tensor_copy(io_f[:, :], io_i[:, :]), reads=[h_io], writes=[h_iof])
        for h in range(4):
            P.add("act", lambda e, h=h: e.activation(qdec[:, h * 128:(h + 1) * 128], io_f[:, :], AF.Exp, scale=LNG[h]),
                  reads=[h_iof], writes=[h_qdec])
        P.add("pool", lambda e: e.iota(io_i[:, 0:1], [[0, 1]], base=127, channel_multiplier=-1), writes=[h_io])
        P.add("dve", lambda e: e.tensor_copy(small[:, 14:15], io_i[:, 0:1]), reads=[h_io], writes=[h_small])
        for h in range(4):
            P.add("act", lambda e, h=h: e.activation(small[:, 8 + h:9 + h], small[:, 14:15], AF.Exp, scale=LNG[h]),
                  reads=[h_small], writes=[h_small])
        P.add("dve", lambda e: e.tensor_scalar(small[:, 8:12], small[:, 8:12], DK ** -0.5, None, ALU.mult),
              reads=[h_small], writes=[h_small])
        P.add("pool", lambda e: e.iota(io_i[:, 0:1], [[0, 1]], base=0, channel_multiplier=1), writes=[h_io])
        P.add("dve", lambda e: e.tensor_copy(small[:, 15:16], io_i[:, 0:1]), reads=[h_io], writes=[h_small])
        P.add("dve", lambda e: e.tensor_scalar(small[:, 12:13], small[:, 15:16], 64.0, None, ALU.is_ge),
              reads=[h_small], writes=[h_small])
        P.add("dve", lambda e: e.scalar_tensor_tensor(small[:, 13:14], small[:, 12:13], -64.0, small[:, 15:16],
                                                      ALU.mult, ALU.add), reads=[h_small], writes=[h_small])
        P.add("dve", lambda e: e.tensor_scalar(small[:, 13:14], small[:, 13:14], -1.0 / 64.0, None, ALU.mult),
              reads=[h_small], writes=[h_small])
        P.add("pool", lambda e: e.tensor_tensor(small[:, 5:6], small[:, 3:4], small[:, 13:14], ALU.pow),
              reads=[h_small], writes=[h_small])
        P.add("dve", lambda e: e.tensor_scalar(small[:, 12:13], small[:, 12:13], 2.0, -1.0, ALU.mult, ALU.add),
              reads=[h_small], writes=[h_small])
        l0 = PCOL["lru_lambda"]
        P.add("act", lambda e: e.activation(small[:, 32:40], pc[:, l0:l0 + 8], AF.Exp, scale=-1.0),
              reads=[h_pc], writes=[h_small])
        P.add("act", lambda e: e.activation(small[:, 40:48], small[:, 32:40], AF.Ln, bias=small[:, 1:2]),
              reads=[h_small], writes=[h_small])
        P.add("dve", lambda e: e.tensor_scalar(small[:, 48:56], small[:, 40:48], -8.0, None, ALU.mult),
              reads=[h_small], writes=[h_small])
        P.add("dve", lambda e: e.tensor_scalar(small[:, 56:64], small[:, 40:48], -16.0, None, ALU.mult),
              reads=[h_small], writes=[h_small])
        P.add("pool", lambda e: e.dma_start(out=wgate[:, 0:1024].rearrange("p (g j) -> p g j", g=8),
                                            in_=Wd["w_rgate"].rearrange("g i j -> i g j")),
              writes=[h_wgate], dma=True)
        P.add("pool", lambda e: e.dma_start(out=wgate[:, 1024:2048].rearrange("p (g j) -> p g j", g=8),
                                            in_=Wd["w_igate"].rearrange("g i j -> i g j")),
              writes=[h_wgate], dma=True)

    w_in_d = Wd["w_in"].rearrange("(k p) f -> p k f", p=128)

    def lru_branch():
        TH = T // 2
        SET = 27 * 1024

        def mk(s):
            o = s * SET
            d = {}
            d["xl"] = R3.view(o, F32, 1028)
            d["xc"] = R3.view(o + 5 * 1024, F32, TH)
            d["hl"] = R3.view(o + 5 * 1024, F32, TH)
            d["ra"] = R3.view(o + 9 * 1024, F32, TH)
            d["ii"] = R3.view(o + 13 * 1024, F32, TH)
            d["tt"] = R3.view(o + 17 * 1024, F32, TH)
            d["gg"] = R3.view(o + 21 * 1024, F32, TH)
            d["xcb"] = R3.view(o + 25 * 1024, BF16, TH)
            return d
        sets = [mk(0), mk(1)]
        hcar = H("lru_carry")
        P.add("dve", lambda e: e.memset(sets[0]["xl"].ap[:, 0:4], 0.0), writes=sets[0]["xl"].hs)

        def ltile(cc):
            return wload_pair(w_in_d[:, :, 3072 + cc * 128:3072 + (cc + 1) * 128],
                              w_in_d[:, :, 4096 + cc * 128:4096 + (cc + 1) * 128], KC, 128)
        tiles = {0: ltile(0)}

        def A(u):
            cc, hf = divmod(u, 2)
            S = sets[hf]
            xl, xc, ra, ii, gg, xcb = S["xl"], S["xc"], S["ra"], S["ii"], S["gg"], S["xcb"]
            wl_t, wl_h = tiles[cc]
            if hf == 0 and cc + 1 < KC:
                tiles[cc + 1] = ltile(cc + 1)
            tbs = [2 * hf, 2 * hf + 1]
            hrd = []
            for k in range(KC):
                for tb in tbs:
                    hrd += hT[(k, tb)].hs
            for r in range(2):
                bb = [nextbank() for _ in range(2)]

                def mm(e, bb=bb, r=r, wl_t=wl_t, tbs=tbs):
                    ins = None
                    for k in range(KC):
                        for j in range(2):
                            ins = e.matmul(banks[bb[j]][:, :], wl_t[:, k, r, :], hT[(k, tbs[j])].ap,
                                           start=(k == 0), stop=(k == KC - 1))
                    return ins
                P.add("pe", mm, reads=hrd + [wl_h], writes=[hbk[i] for i in bb])
                for j in range(2):
                    if r == 0:
                        P.add("act", lambda e, j=j, bb=bb, xl=xl: e.activation(
                            xl.ap[:, 3 + j * TB:3 + (j + 1) * TB], banks[bb[j]][:, :], AF.Copy),
                            reads=[hbk[bb[j]]], writes=xl.hs)
                    else:
                        P.add("act", lambda e, j=j, bb=bb, gg=gg: e.activation(
                            gg.ap[:, j * TB:(j + 1) * TB], banks[bb[j]][:, :], AF.Gelu),
                            reads=[hbk[bb[j]]], writes=gg.hs)
            if hf == 1:
                x0 = sets[0]["xl"]
                P.add("act", lambda e, x0=x0, xl=xl: e.activation(xl.ap[:, 0:3], x0.ap[:, TH:TH + 3], AF.Copy),
                      reads=x0.hs, writes=xl.hs)
            P.add("act", lambda e, cc=cc, xc=xc, xl=xl: e.activation(xc.ap, xl.ap[:, 3:3 + TH], AF.Identity,
                                                                    bias=pcol("conv_b", cc), scale=pcol("conv_w", 24 + cc)),
                  reads=xl.hs + [h_pc], writes=xc.hs)
            for s in range(3):
                P.add("dve", lambda e, cc=cc, s=s, xc=xc, xl=xl: e.scalar_tensor_tensor(
                    xc.ap, xl.ap[:, s:s + TH], pcol("conv_w", s * 8 + cc), xc.ap, ALU.mult, ALU.add),
                    reads=xl.hs + xc.hs + [h_pc], writes=xc.hs)

        def A2(u):
            cc, hf = divmod(u, 2)
            S = sets[hf]
            xl, xc, ra, ii, gg, xcb = S["xl"], S["xc"], S["ra"], S["ii"], S["gg"], S["xcb"]
            P.add("act", lambda e, xc=xc, xcb=xcb: e.activation(xcb.ap, xc.ap, AF.Copy), reads=xc.hs, writes=xcb.hs)
            for gi in range(2):
                bb = [nextbank() for _ in range(2)]

                def mmg(e, bb=bb, gi=gi, cc=cc, xcb=xcb):
                    ins = None
                    for j in range(2):
                        ins = e.matmul(banks[bb[j]][:, :], wgate[:, gi * 1024 + cc * 128:gi * 1024 + (cc + 1) * 128],
                                       xcb.ap[:, j * TB:(j + 1) * TB], start=True, stop=True)
                    return ins
                P.add("pe", mmg, reads=xcb.hs + [h_wgate], writes=[hbk[i] for i in bb])
                dst = ra if gi == 0 else ii
                bname = "b_rgate" if gi == 0 else "b_igate"
                for j in range(2):
                    P.add("act", lambda e, j=j, bb=bb, dst=dst, bname=bname, cc=cc: e.activation(
                        dst.ap[:, j * TB:(j + 1) * TB], banks[bb[j]][:, :], AF.Sigmoid, bias=pcol(bname, cc)),
                        reads=[hbk[bb[j]], h_pc], writes=dst.hs)

        def Bk_(u):
            cc, hf = divmod(u, 2)
            S = sets[hf]
            xc, hl, ra, ii, tt_, gg = S["xc"], S["hl"], S["ra"], S["ii"], S["tt"], S["gg"]
            P.add("pool", lambda e: e.tensor_tensor(ii.ap, ii.ap, xc.ap, ALU.mult), reads=ii.hs + xc.hs, writes=ii.hs)
            P.add("act", lambda e: e.activation(tt_.ap, ra.ap, AF.Exp, scale=small[:, 56 + cc:57 + cc]),
                  reads=ra.hs + [h_small], writes=tt_.hs)
            P.add("act", lambda e: e.activation(ra.ap, ra.ap, AF.Exp, scale=small[:, 48 + cc:49 + cc]),
                  reads=ra.hs + [h_small], writes=ra.hs)
            P.add("act", lambda e: e.activation(tt_.ap, tt_.ap, AF.Ln, bias=small[:, 1:2], scale=-1.0),
                  reads=tt_.hs + [h_small], writes=tt_.hs)
            P.add("act", lambda e: e.activation(tt_.ap, tt_.ap, AF.Exp, scale=0.5), reads=tt_.hs, writes=tt_.hs)

        def Bb(u):
            cc, hf = divmod(u, 2)
            S = sets[hf]
            xc, hl, ra, ii, tt_, gg = S["xc"], S["hl"], S["ra"], S["ii"], S["tt"], S["gg"]
            P.add("dve", lambda e: e.scalar_tensor_tensor(tt_.ap, ii.ap, 1.0, tt_.ap, ALU.mult, ALU.mult),
                  reads=tt_.hs + ii.hs, writes=tt_.hs)
            if hf == 0:
                P.add("dve", lambda e: e.tensor_tensor_scan(hl.ap, ra.ap, tt_.ap, 0.0, ALU.mult, ALU.add),
                      reads=ra.hs + tt_.hs + xc.hs, writes=hl.hs)
                P.add("dve", lambda e: e.tensor_copy(small[:, 24:25], hl.ap[:, TH - 1:TH]), reads=hl.hs, writes=[hcar])
            else:
                P.add("dve", lambda e: e.tensor_tensor_scan(hl.ap, ra.ap, tt_.ap, small[:, 24:25], ALU.mult, ALU.add),
                      reads=ra.hs + tt_.hs + xc.hs + [hcar], writes=hl.hs)
            P.add("pool", lambda e: e.tensor_tensor(BT[cc].ap[:, hf * TH:(hf + 1) * TH], hl.ap, gg.ap, ALU.mult),
                  reads=hl.hs + gg.hs, writes=BT[cc].hs)

        NU = 2 * KC
        A(0)
        A2(0)
        for u in range(NU):
            if u + 1 < NU:
                A(u + 1)
            Bk_(u)
            if u + 1 < NU:
                A2(u + 1)
            Bb(u)

    def rope_tables():
        A = R3.view(0, F32, T)
        B = R3.view(8192, F32, T)
        Bi = R3.view(8192, I32, T)
        C = R3.view(16384, F32, T)
        invf = small[:, 5:6]

        def wrap(gt):
            if gt:
                P.add("dve", lambda e: e.tensor_scalar(C.ap, A.ap, math.pi, -TWO_PI, ALU.is_gt, ALU.mult),
                      reads=A.hs, writes=C.hs)
            else:
                P.add("dve", lambda e: e.tensor_scalar(C.ap, A.ap, -math.pi, TWO_PI, ALU.is_lt, ALU.mult),
                      reads=A.hs, writes=C.hs)
            P.add("dve", lambda e: e.scalar_tensor_tensor(A.ap, C.ap, 1.0, A.ap, ALU.mult, ALU.add),
                  reads=A.hs + C.hs, writes=A.hs)

        def clamp():
            P.add("dve", lambda e: e.tensor_scalar(A.ap, A.ap, PI_LO, -PI_LO, ALU.min, ALU.max), reads=A.hs, writes=A.hs)
        P.add("pool", lambda e: e.iota(Bi.ap, [[1, T]], base=0, channel_multiplier=0), writes=Bi.hs)
        P.add("dve", lambda e: e.tensor_copy(A.ap, Bi.ap), reads=Bi.hs, writes=A.hs)
        P.add("dve", lambda e: e.tensor_scalar(A.ap, A.ap, invf, None, ALU.mult), reads=A.hs + [h_small], writes=A.hs)
        P.add("dve", lambda e: e.tensor_scalar(Bi.ap, A.ap, 1.0 / TWO_PI, None, ALU.mult), reads=A.hs, writes=Bi.hs)
        P.add("dve", lambda e: e.tensor_copy(B.ap, Bi.ap), reads=Bi.hs, writes=B.hs)
        P.add("dve", lambda e: e.scalar_tensor_tensor(A.ap, B.ap, -CW1, A.ap, ALU.mult, ALU.add), reads=A.hs + B.hs, writes=A.hs)
        P.add("dve", lambda e: e.scalar_tensor_tensor(A.ap, B.ap, -CW2, A.ap, ALU.mult, ALU.add), reads=A.hs + B.hs, writes=A.hs)
        wrap(True)
        wrap(False)
        clamp()
        P.add("act", lambda e: e.activation(B.ap, A.ap, AF.Sin), reads=A.hs, writes=B.hs)
        P.add("dve", lambda e: e.tensor_scalar(A.ap, A.ap, math.pi / 2.0, None, ALU.add), reads=A.hs, writes=A.hs)
        wrap(True)
        clamp()
        P.add("act", lambda e: e.activation(A.ap, A.ap, AF.Sin), reads=A.hs, writes=A.hs)
        P.add("dve", lambda e: e.tensor_scalar(B.ap, B.ap, small[:, 12:13], None, ALU.mult),
              reads=B.hs + [h_small], writes=B.hs)
        return A, B

    def retention_branch():
        cosT, sinT = rope_tables()
        if stage == 25:
            dbg("cos", cosT, F32)
            dbg("sin", sinT, F32)
            return
        t1 = R3.view(16 * 1024, F32, TB)
        t2 = R3.view(18 * 1024, F32, TB)
        t1k = R3.view(20 * 1024, F32, TB)
        t2k = R3.view(22 * 1024, F32, TB)
        qb = R3.view(24 * 1024, BF16, T)
        qdb = R3.view(28 * 1024, BF16, T)
        kb = R3.view(32 * 1024, BF16, T)
        v_sb = [R3.view(36 * 1024 + i * 512, BF16, 256) for i in range(4)]
        G2 = [R3.view(38 * 1024 + i * 1024, F32, 256) for i in range(4)]
        sg = [R3.view(42 * 1024 + i * 1024, F32, 256) for i in range(2)]
        ret_sb = [R3.view(44 * 1024 + i * 1024, F32, 256) for i in range(4)]
        A_tok = [R3.view(48 * 1024 + i * 512, BF16, 256) for i in range(2)]
        sc_sb = [R3.view(49 * 1024 + i * 256, BF16, 128) for i in range(4)]
        state = R3.view(50 * 1024, F32, 256)
        state_bf = [R3.view(51 * 1024 + i * 512, BF16, 256) for i in range(4)]
        ktok = [R3.view(53 * 1024 + i * 256, BF16, 128) for i in range(2)]
        junk = R3.view(53 * 1024 + 512, BF16, 256)
        hst = [H("st%d" % i) for i in range(4)]
        P.add("sp", lambda e: e.dma_start(out=bcrow[:, :], in_=bass.AP(Wd["ret_gn"].tensor, 0, [[0, 128], [1, D]])),
              writes=[h_bcrow], dma=True)

        def hload(h):
            c = h * 128
            wq = wload(w_in_d[:, :, c:c + 128], KC, 128)
            wqs_ap, wqs_h, ev = walloc(KC * 128)
            wqs = wqs_ap.rearrange("p (k f) -> p k f", k=KC)
            P.add("pool", lambda e: e.dma_start(out=wqs[:, :, 0:64], in_=w_in_d[:, :, c + 64:c + 128]),
                  writes=[wqs_h] + ev, dma=True)
            P.add("pool", lambda e: e.dma_start(out=wqs[:, :, 64:128], in_=w_in_d[:, :, c:c + 64]),
                  writes=[wqs_h], dma=True)
            wk = wload(w_in_d[:, :, 512 + c:512 + c + 128], KC, 128)
            wks_ap, wks_h, ev2 = walloc(KC * 128)
            wks = wks_ap.rearrange("p (k f) -> p k f", k=KC)
            P.add("pool", lambda e: e.dma_start(out=wks[:, :, 0:64], in_=w_in_d[:, :, 512 + c + 64:512 + c + 128]),
                  writes=[wks_h] + ev2, dma=True)
            P.add("pool", lambda e: e.dma_start(out=wks[:, :, 64:128], in_=w_in_d[:, :, 512 + c:512 + c + 64]),
                  writes=[wks_h], dma=True)
            wvg = wload_pair(w_in_d[:, :, 1024 + h * 256:1024 + (h + 1) * 256],
                             w_in_d[:, :, 2048 + h * 256:2048 + (h + 1) * 256], KC, 256)
            return wq, (wqs, wqs_h), wk, (wks, wks_h), wvg

        nxt = hload(0)
        for h in range(4):
            (wq_t, wq_h), (wqs_t, wqs_h), (wk_t, wk_h), (wks_t, wks_h), (wvg_t, wvg_h) = nxt
            for tb in range(NTB):
                hrd = []
                for k in range(KC):
                    hrd += hT[(k, tb)].hs
                bq = []
                for (wt_, wh_) in ((wq_t, wq_h), (wqs_t, wqs_h), (wk_t, wk_h), (wks_t, wks_h)):
                    b = nextbank()
                    bq.append(b)

                    def mm(e, b=b, wt_=wt_, tb=tb):
                        ins = None
                        for k in range(KC):
                            ins = e.matmul(banks[b][:, :], wt_[:, k, :], hT[(k, tb)].ap, start=(k == 0), stop=(k == KC - 1))
                        return ins
                    P.add("pe", mm, reads=hrd + [wh_], writes=[hbk[b]])
                cs = cosT.ap[:, tb * TB:(tb + 1) * TB]
                sn = sinT.ap[:, tb * TB:(tb + 1) * TB]
                P.add("dve", lambda e, b=bq[0], cs=cs: e.tensor_tensor(t1.ap, banks[b][:, :], cs, ALU.mult),
                      reads=[hbk[bq[0]]] + cosT.hs, writes=t1.hs)
                P.add("dve", lambda e, b=bq[1], sn=sn: e.tensor_tensor(t2.ap, banks[b][:, :], sn, ALU.mult),
                      reads=[hbk[bq[1]]] + sinT.hs, writes=t2.hs)
                P.add("dve", lambda e, b=bq[2], cs=cs: e.tensor_tensor(t1k.ap, banks[b][:, :], cs, ALU.mult),
                      reads=[hbk[bq[2]]] + cosT.hs, writes=t1k.hs)
                P.add("dve", lambda e, b=bq[3], sn=sn: e.tensor_tensor(t2k.ap, banks[b][:, :], sn, ALU.mult),
                      reads=[hbk[bq[3]]] + sinT.hs, writes=t2k.hs)
                P.add("pool", lambda e: e.tensor_tensor(t1.ap, t1.ap, t2.ap, ALU.add), reads=t1.hs + t2.hs, writes=t1.hs)
                P.add("pool", lambda e, tb=tb: e.tensor_copy(qb.ap[:, tb * TB:(tb + 1) * TB], t1.ap),
                      reads=t1.hs, writes=qb.hs)
                qd_in1 = bass.AP(qdec, h * 128, [[512, 128], [0, 4], [1, 128]])
                P.add("pool", lambda e, tb=tb, qd_in1=qd_in1: e.tensor_tensor(
                    qdb.ap[:, tb * TB:(tb + 1) * TB].rearrange("p (a i) -> p a i", a=4),
                    t1.ap.rearrange("p (a i) -> p a i", a=4), qd_in1, ALU.mult),
                    reads=t1.hs + [h_qdec], writes=qdb.hs)
                P.add("pool", lambda e, tb=tb: e.tensor_tensor(kb.ap[:, tb * TB:(tb + 1) * TB], t1k.ap, t2k.ap, ALU.add),
                      reads=t1k.hs + t2k.hs, writes=kb.hs)
            if h + 1 < 4:
                nxt = hload(h + 1)
            P.add("dve", lambda e: e.memset(state.ap, 0.0), writes=state.hs)

            def F(c, h=h, wvg_t=wvg_t, wvg_h=wvg_h):
                cols = slice(c * 128, (c + 1) * 128)
                b = nextbank()

                def mmv(e, b=b):
                    ins = None
                    for k in range(KC):
                        ins = e.matmul(banks[b][:, :], hT_all.ap[:, k, cols], wvg_t[:, k, :, :].rearrange("p r f -> p (r f)"),
                                       start=(k == 0), stop=(k == KC - 1))
                    return ins
                hrd = []
                for k in range(KC):
                    hrd += hT[(k, c // 4)].hs
                P.add("pe", mmv, reads=hrd + [wvg_h], writes=[hbk[b]])
                if F_LEVEL < 2:
                    return
                P.add("act", lambda e, b=b: e.activation(v_sb[c % 4].ap, banks[b][:, 0:256], AF.Copy),
                      reads=[hbk[b]], writes=v_sb[c % 4].hs)
                if F_LEVEL < 2.3:
                    return
                P.add("act", lambda e, b=b: e.activation(sg[c % 2].ap, banks[b][:, 256:512], SG_FUNC),
                      reads=[hbk[b]], writes=sg[c % 2].hs)
                if F_LEVEL < 2.6:
                    return
                P.add("pool", lambda e: e.tensor_tensor(G2[c % 4].ap, sg[c % 2].ap, bcrow[:, h * 256:(h + 1) * 256], ALU.mult),
                      reads=sg[c % 2].hs + [h_bcrow], writes=G2[c % 4].hs)
                if F_LEVEL < 3:
                    return
                b2 = nextbank()
                P.add("pe", lambda e, b2=b2: e.matmul(banks[b2][:, 0:128], kb.ap[:, cols], qb.ap[:, cols], start=True, stop=True),
                      reads=kb.hs + qb.hs, writes=[hbk[b2]])
                P.add("dve", lambda e, b2=b2: e.tensor_tensor(sc_sb[c % 4].ap, banks[b2][:, 0:128],
                                                              decT[:, h * 128:(h + 1) * 128], ALU.mult),
                      reads=[hbk[b2], h_dec], writes=sc_sb[c % 4].hs)
                if F_LEVEL < 4:
                    return
                b3 = nextbank()
                b3v = banks[b3][:, :].bitcast(BF16)
                P.add("pe", lambda e, b3v=b3v: e.transpose(b3v[:, 0:128], kb.ap[:, cols], identb[:, :]),
                      reads=kb.hs + [h_identb], writes=[hbk[b3]])
                P.add("act", lambda e, b3v=b3v: e.activation(ktok[c % 2].ap, b3v[:, 0:128], AF.Identity, scale=small[:, 8 + h:9 + h]),
                      reads=[hbk[b3], h_small], writes=ktok[c % 2].hs)

            def Kst(c, h=h):
                b2 = nextbank()
                P.add("pe", lambda e, b2=b2: e.matmul(banks[b2][:, 0:256], ktok[c % 2].ap, v_sb[c % 4].ap, start=True, stop=True),
                      reads=ktok[c % 2].hs + v_sb[c % 4].hs, writes=[hbk[b2]])
                if c < 15:
                    P.add("dve", lambda e, b2=b2: e.scalar_tensor_tensor(state.ap, state.ap, CDEC[h], banks[b2][:, 0:256],
                                                                         ALU.mult, ALU.add),
                          reads=state.hs + [hbk[b2]], writes=state.hs)
                    P.add("dve", lambda e: e.tensor_copy(state_bf[(c + 1) % 4].ap, state.ap),
                          reads=state.hs, writes=state_bf[(c + 1) % 4].hs)

            def M(c, h=h):
                cols = slice(c * 128, (c + 1) * 128)
                b = nextbank()

                def mmr(e, b=b):
                    ins = e.matmul(banks[b][:, 0:256], sc_sb[c % 4].ap, v_sb[c % 4].ap, start=True, stop=(c == 0))
                    if c > 0:
                        ins = e.matmul(banks[b][:, 0:256], qdb.ap[:, cols], state_bf[c % 4].ap, start=False, stop=True)
                    return ins
                P.add("pe", mmr, reads=sc_sb[c % 4].hs + v_sb[c % 4].hs + qdb.hs + (state_bf[c % 4].hs if c > 0 else []),
                      writes=[hbk[b]])
                s0 = 64 + (c % 4) * 8
                P.add("act", lambda e, b=b: e.activation(ret_sb[c % 4].ap, banks[b][:, 0:256], AF.Identity,
                                                         accum_out=small[:, s0:s0 + 1]),
                      reads=[hbk[b]], writes=ret_sb[c % 4].hs + [hst[c % 4]])
                P.add("act", lambda e, b=b: e.activation(junk.ap, banks[b][:, 0:256], AF.Square,
                                                         accum_out=small[:, s0 + 1:s0 + 2]),
                      reads=[hbk[b]], writes=junk.hs + [hst[c % 4]])

            def B1(c, h=h):
                s0 = 64 + (c % 4) * 8
                hs_ = [hst[c % 4]]

                def col(i):
                    return small[:, s0 + i:s0 + i + 1]
                P.add("dve", lambda e: e.tensor_scalar(col(2), col(0), 1.0 / 256.0, None, ALU.mult), reads=hs_, writes=hs_)
                P.add("dve", lambda e: e.tensor_tensor(col(3), col(2), col(2), ALU.mult), reads=hs_, writes=hs_)
                P.add("dve", lambda e: e.scalar_tensor_tensor(col(4), col(1), 1.0 / 256.0, col(3), ALU.mult, ALU.subtract),
                      reads=hs_, writes=hs_)
                P.add("dve", lambda e: e.tensor_scalar(col(7), col(4), EPS, None, ALU.add), reads=hs_, writes=hs_)
                P.add("pool", lambda e: e.tensor_tensor(col(5), col(7), small[:, 2:3], ALU.pow),
                      reads=hs_ + [h_small], writes=hs_)
                P.add("dve", lambda e: e.scalar_tensor_tensor(col(6), col(2), -1.0, col(5), ALU.mult, ALU.mult),
                      reads=hs_, writes=hs_)

            def B2(c, h=h):
                s0 = 64 + (c % 4) * 8
                hs_ = [hst[c % 4]]

                def col(i):
                    return small[:, s0 + i:s0 + i + 1]
                P.add("act", lambda e: e.activation(ret_sb[c % 4].ap, ret_sb[c % 4].ap, AF.Identity, bias=col(6), scale=col(5)),
                      reads=ret_sb[c % 4].hs + hs_, writes=ret_sb[c % 4].hs)
                P.add("pool", lambda e: e.tensor_tensor(A_tok[c % 2].ap, ret_sb[c % 4].ap, G2[c % 4].ap, ALU.mult),
                      reads=ret_sb[c % 4].hs + G2[c % 4].hs, writes=A_tok[c % 2].hs)

            def B3(c, h=h):
                cols = slice(c * 128, (c + 1) * 128)
                b = nextbank()
                bv = banks[b][:, :].bitcast(BF16)

                def trA(e, bv=bv):
                    ins = None
                    for j in range(2):
                        ins = e.transpose(bv[:, j * 128:(j + 1) * 128], A_tok[c % 2].ap[:, j * 128:(j + 1) * 128], identb[:, :])
                    return ins
                P.add("pe", trA, reads=A_tok[c % 2].hs + [h_identb], writes=[hbk[b]])
                for j in range(2):
                    dstv = AT[2 * h + j]
                    P.add("act", lambda e, bv=bv, dstv=dstv, j=j: e.activation(dstv.ap[:, cols], bv[:, j * 128:(j + 1) * 128], AF.Copy),
                          reads=[hbk[b]], writes=dstv.hs)

            for step in range(16 + 5):
                for si, fn_ in ((5, B3), (4, B2), (3, B1), (2, M)):
                    c = step - si
                    if 0 <= c < 16:
                        fn_(c)
                if step < 16:
                    F(step)
                if 0 <= step - 1 < 16:
                    Kst(step - 1)

    def mixer_tail():
        merged = {(oc, tb): R3.view((oc * T + tb * TB) * 2, BF16, TB) for oc in range(KC) for tb in range(NTB)}
        g_sb = [R3.view(32 * 1024 + i * 2048, F32, TB) for i in range(NTB)]
        m1 = [R3.view(40 * 1024 + i * 2048, F32, TB) for i in range(NTB)]
        g2_sb = [R3.view(48 * 1024 + i * 2048, F32, TB) for i in range(NTB)]
        hTh = allhs(hT)
        ATh = []
        BTh = []
        for i in range(KC):
            ATh += AT[i].hs
            BTh += BT[i].hs
        wro_d = Wd["w_ret_o"].rearrange("(k p) f -> p k f", p=128)
        wlo_d = Wd["w_lru_o"].rearrange("(k p) f -> p k f", p=128)
        wbg_d = Wd["w_branch_gate"].rearrange("(k p) f -> p k f", p=128)

        def tload(oc):
            sl = slice(oc * 128, (oc + 1) * 128)
            sl2 = slice(1024 + oc * 128, 1024 + (oc + 1) * 128)
            return (wload(wro_d[:, :, sl], KC, 128), wload(wbg_d[:, :, sl], KC, 128),
                    wload(wlo_d[:, :, sl], KC, 128), wload(wbg_d[:, :, sl2], KC, 128))
        nxt = tload(0)
        for oc in range(KC):
            (wro, wro_h), (wg1, wg1_h), (wlo, wlo_h), (wg2, wg2_h) = nxt
            if oc + 1 < KC:
                nxt = tload(oc + 1)

            def proj(wt_, wh_, rhs_list, rhs_hs):
                bb = [nextbank() for _ in range(NTB)]

                def mm(e, bb=bb, wt_=wt_, rhs_list=rhs_list):
                    ins = None
                    for k in range(KC):
                        for tb in range(NTB):
                            ins = e.matmul(banks[bb[tb]][:, :], wt_[:, k, :], rhs_list(k, tb), start=(k == 0), stop=(k == KC - 1))
                    return ins
                P.add("pe", mm, reads=rhs_hs + [wh_], writes=[hbk[i] for i in bb])
                return bb
            by = proj(wro, wro_h, lambda k, tb: AT[k].ap[:, tb * TB:(tb + 1) * TB], ATh)
            bg = proj(wg1, wg1_h, lambda k, tb: hT[(k, tb)].ap, hTh)
            for tb in range(NTB):
                P.add("act", lambda e, tb=tb, bg=bg, oc=oc: e.activation(g_sb[tb].ap, banks[bg[tb]][:, :], AF.Sigmoid,
                                                                         bias=pcol("b_branch_gate", oc)),
                      reads=[hbk[bg[tb]], h_pc], writes=g_sb[tb].hs)
                P.add("dve", lambda e, tb=tb, by=by: e.tensor_tensor(m1[tb].ap, banks[by[tb]][:, :], g_sb[tb].ap, ALU.mult),
                      reads=[hbk[by[tb]]] + g_sb[tb].hs, writes=m1[tb].hs)
            by2 = proj(wlo, wlo_h, lambda k, tb: BT[k].ap[:, tb * TB:(tb + 1) * TB], BTh)
            bg2 = proj(wg2, wg2_h, lambda k, tb: hT[(k, tb)].ap, hTh)
            for tb in range(NTB):
                P.add("act", lambda e, tb=tb, bg2=bg2, oc=oc: e.activation(g2_sb[tb].ap, banks[bg2[tb]][:, :], AF.Sigmoid,
                                                                           bias=pcol("b_branch_gate", 8 + oc)),
                      reads=[hbk[bg2[tb]], h_pc], writes=g2_sb[tb].hs)
                P.add("dve", lambda e, tb=tb, by2=by2: e.scalar_tensor_tensor(g2_sb[tb].ap, banks[by2[tb]][:, :], 1.0, g2_sb[tb].ap,
                                                                            ALU.mult, ALU.mult),
                      reads=[hbk[by2[tb]]] + g2_sb[tb].hs, writes=g2_sb[tb].hs)
                P.add("pool", lambda e, tb=tb, oc=oc: e.tensor_tensor(merged[(oc, tb)].ap, m1[tb].ap, g2_sb[tb].ap, ALU.add),
                      reads=m1[tb].hs + g2_sb[tb].hs, writes=merged[(oc, tb)].hs)
        wo_t, wo_h = wload(Wd["w_out"].rearrange("(k p) f -> p k f", p=128), KC, D)
        unpark()
        mh = allhs(merged)
        for oc in range(KC):
            bb = [nextbank() for _ in range(NTB)]

            def mmo(e, bb=bb, oc=oc):
                ins = None
                for k in range(KC):
                    for tb in range(NTB):
                        ins = e.matmul(banks[bb[tb]][:, :], wo_t[:, k, oc * 128:(oc + 1) * 128], merged[(k, tb)].ap,
                                       start=(k == 0), stop=(k == KC - 1))
                return ins
            P.add("pe", mmo, reads=mh + [wo_h], writes=[hbk[i] for i in bb])
            for tb in range(NTB):
                P.add("dve", lambda e, tb=tb, bb=bb, oc=oc: e.scalar_tensor_tensor(
                    xT[(oc, tb)].ap, banks[bb[tb]][:, :], 1.0, xT[(oc, tb)].ap, ALU.mult, ALU.add),
                    reads=[hbk[bb[tb]]] + xT[(oc, tb)].hs, writes=xT[(oc, tb)].hs)

    def unpark():
        for c in range(KC):
            whs = []
            for tb in range(NTB):
                whs += xT[(c, tb)].hs
            P.add("sp", lambda e, c=c: e.dma_start(out=xT_all.ap[:, c, :], in_=park_d[:, c * T:(c + 1) * T]),
                  reads=[hpark], writes=whs, dma=True)

    def mixer():
        if stage not in (21,):
            mixer_consts()
        rmsnorm_T("mix_norm")
        for c in range(KC):
            rhs = []
            for tb in range(NTB):
                rhs += xT[(c, tb)].hs
            P.add("sp", lambda e, c=c: e.dma_start(out=park_d[:, c * T:(c + 1) * T], in_=xT_all.ap[:, c, :]),
                  reads=rhs, writes=[hpark], dma=True)
        if stage in (21, 24):
            unpark()
            return
        if stage not in (23, 25, 26):
            lru_branch()
            for cc in range(KC):
                dbg("BT%d" % cc, BT[cc], BF16)
        if stage == 22:
            unpark()
            return
        retention_branch()
        for ec in range(KC):
            dbg("AT%d" % ec, AT[ec], BF16)
        if stage in (23, 25, 26):
            unpark()
            return
        mixer_tail()

    setup_small()
    setup_consts()
    load_x()
    if stage >= 1:
        ffn("ffn1_w1", "ffn1_w3", "ffn1_w2", "ffn1_norm")
    if stage >= 2 and stage not in (30, 31):
        mixer()
    if stage >= 3 and stage not in (21, 22, 23, 24, 25, 26):
        xattn()
    if stage >= 4 and stage not in (30, 31, 21, 22, 23, 24, 25, 26):
        ffn("ffn2_w1", "ffn2_w3", "ffn2_w2", "ffn2_norm")
    store_out(final_norm=(stage >= 5 and stage not in (30, 31, 21, 22, 23, 24, 25, 26)))
    if DEBUG:
        last = P.ops[-1]
        for op in P.ops:
            if op.dma and op.eng == "sp":
                last.deps.append(op)
        DBG_NAMES[:] = dbg_list
    P.emit(nc)
    return nc


_W_NAMES = [n for n, _ in WSHAPES]


def run(inputs, stage=99, ncores=8, trace=False):
    nc = build(stage)
    x = np.ascontiguousarray(np.asarray(inputs["x"], dtype=np.float32))
    mem = np.ascontiguousarray(np.asarray(inputs["mem"], dtype=np.float32))
    shared = {}
    for n, shp in WSHAPES:
        shared[n] = np.ascontiguousarray(np.asarray(inputs[n], dtype=np.float32).reshape(shp))
    in_maps = []
    for b in range(ncores):
        m = {"x": x[b], "mem": mem[b]}
        m.update(shared)
        in_maps.append(m)
    res = run_bass_kernel_spmd(nc, in_maps, core_ids=list(range(ncores)), trace=trace)
    out = np.stack([np.asarray(r["out"]) for r in res.results], axis=0)
    if DEBUG:
        res.dbg = {n: np.asarray(res.results[0][n]) for n in DBG_NAMES}
    return out.astype(np.float32), res


def kernel(**inputs):
    out, _ = run(inputs, stage=99, ncores=8)
    return out
```
